# Optimizing a Trainium2 kernel written in Bass

```python
import math
import jax
import jax.numpy as jnp
from jax import lax
import numpy as np


D_MODEL = 1024
BATCH = 8
SEQ = 4096
DEPTH = 4

N_MIXERS = 3
ROPE_THETA = 500000.0
EPS = 1e-6
Q_BLOCK = 128
MASK_VALUE = -1e30

A_HEADS = 8
A_HEAD_DIM = D_MODEL // A_HEADS // 2
A_V_DIM = 2 * A_HEAD_DIM
A_ROT = A_HEAD_DIM // 4

B_HEADS = 16
B_Q_RANK = 384
B_KV_RANK = 256
B_NOPE = 64
B_ROPE = 32
B_V = 64

C_GROUPS = ((128, 1), (512, 4), (2048, 16))
C_HEADS = 16
C_HEAD_DIM = 64
C_ROT = C_HEAD_DIM // 4
C_BLOCK = 64

FFN_DIM = -(-8 * D_MODEL // (3 * 256)) * 256

N_A = (DEPTH + 2) // 3
N_B = (DEPTH + 1) // 3
N_C = DEPTH // 3

kernel_name = 'hybrid_interleaved_diff_mla_dilated_encoder'


def rms_norm(x, g):
    xf = x.astype(jnp.float32)
    y = xf * lax.rsqrt(jnp.mean(xf * xf, axis=-1, keepdims=True) + EPS) * g.astype(jnp.float32)
    return y.astype(x.dtype)


def rope_tables(seq_len, rot_dim):
    pos = jnp.arange(seq_len, dtype=jnp.float32)
    inv = ROPE_THETA ** (-jnp.arange(0, rot_dim, 2, dtype=jnp.float32) / rot_dim)
    ang = pos[:, None] * inv[None, :]
    return jnp.cos(ang), jnp.sin(ang)


def apply_rope(x, cos, sin):
    r = cos.shape[-1]
    shape = (x.shape[1],) + (1,) * (x.ndim - 3) + (r,)
    c = cos.reshape(shape)
    s = sin.reshape(shape)
    xf = x[..., :2 * r].astype(jnp.float32)
    x1, x2 = xf[..., :r], xf[..., r:]
    rot = jnp.concatenate([x1 * c - x2 * s, x2 * c + x1 * s], axis=-1).astype(x.dtype)
    return jnp.concatenate([rot, x[..., 2 * r:]], axis=-1)


def _to_blocks(t):
    b, h, s, d = t.shape
    return t.reshape(b, h, s // Q_BLOCK, Q_BLOCK, d).transpose(2, 0, 1, 3, 4)


def _from_blocks(t):
    nb, b, h, q, d = t.shape
    return t.transpose(1, 2, 0, 3, 4).reshape(b, h, nb * q, d)


def diff_attention(h, w_qkv, lam_q1, lam_k1, lam_q2, lam_k2, subln, w_o, lambda_init, cos, sin):
    B, S, _ = h.shape
    qk_w = 2 * A_HEADS * A_HEAD_DIM
    qkv = h @ w_qkv
    q = qkv[..., :qk_w].reshape(B, S, 2 * A_HEADS, A_HEAD_DIM)
    k = qkv[..., qk_w:2 * qk_w].reshape(B, S, 2 * A_HEADS, A_HEAD_DIM)
    v = qkv[..., 2 * qk_w:].reshape(B, S, A_HEADS, A_V_DIM)
    q = (apply_rope(q, cos, sin) * (A_HEAD_DIM ** -0.5)).transpose(0, 2, 1, 3)
    k = apply_rope(k, cos, sin).transpose(0, 2, 1, 3)
    v = v.transpose(0, 2, 1, 3)
    lam = (jnp.exp(jnp.sum(lam_q1.astype(jnp.float32) * lam_k1.astype(jnp.float32)))
           - jnp.exp(jnp.sum(lam_q2.astype(jnp.float32) * lam_k2.astype(jnp.float32)))
           + lambda_init)

    def block(qb):
        s = jnp.einsum('bhqd,bhkd->bhqk', qb, k).astype(jnp.float32)
        p = jax.nn.softmax(s, axis=-1).reshape(B, A_HEADS, 2, qb.shape[2], S)
        a = p[:, :, 0] - lam * p[:, :, 1]
        return jnp.einsum('bhqk,bhkd->bhqd', a.astype(v.dtype), v)

    o = _from_blocks(lax.map(block, _to_blocks(q)))
    o = rms_norm(o, subln) * (1.0 - lambda_init)
    o = o.transpose(0, 2, 1, 3).reshape(B, S, A_HEADS * A_V_DIM)
    return o @ w_o


def latent_attention(h, w_a, q_norm, kv_norm, w_qb, w_kvb, w_o, cos, sin):
    B, S, _ = h.shape
    a = h @ w_a
    q_lat = rms_norm(a[..., :B_Q_RANK], q_norm)
    kv_lat = rms_norm(a[..., B_Q_RANK:B_Q_RANK + B_KV_RANK], kv_norm)
    k_rope = apply_rope(a[..., B_Q_RANK + B_KV_RANK:], cos, sin)
    scale = (B_NOPE + B_ROPE) ** -0.5
    q = (q_lat @ w_qb).reshape(B, S, B_HEADS, B_NOPE + B_ROPE)
    q_nope = (q[..., :B_NOPE] * scale).transpose(0, 2, 1, 3)
    q_rope = (apply_rope(q[..., B_NOPE:], cos, sin) * scale).transpose(0, 2, 1, 3)
    kv = (kv_lat @ w_kvb).reshape(B, S, B_HEADS, B_NOPE + B_V)
    k_nope = kv[..., :B_NOPE].transpose(0, 2, 1, 3)
    v = kv[..., B_NOPE:].transpose(0, 2, 1, 3)

    def block(qs):
        qn, qr = qs
        s = (jnp.einsum('bhqd,bhkd->bhqk', qn, k_nope)
             + jnp.einsum('bhqr,bkr->bhqk', qr, k_rope)).astype(jnp.float32)
        p = jax.nn.softmax(s, axis=-1)
        return jnp.einsum('bhqk,bhkd->bhqd', p.astype(v.dtype), v)

    o = _from_blocks(lax.map(block, (_to_blocks(q_nope), _to_blocks(q_rope))))
    o = o.transpose(0, 2, 1, 3).reshape(B, S, B_HEADS * B_V)
    return o @ w_o


def _dilated_group(q, k, v, window, dilation):
    B, S, H, hd = q.shape
    radius = (window // 2) // dilation
    L = S // dilation
    nb = -(-L // C_BLOCK)
    Lp = nb * C_BLOCK

    def phase(t):
        t = t.reshape(B, L, dilation, H, hd).transpose(0, 2, 3, 1, 4)
        return jnp.pad(t, ((0, 0), (0, 0), (0, 0), (0, Lp - L), (0, 0)))

    def band(t):
        t = jnp.pad(phase(t), ((0, 0), (0, 0), (0, 0), (C_BLOCK, C_BLOCK), (0, 0)))
        t = t.reshape(B, dilation, H, nb + 2, C_BLOCK, hd)
        return jnp.concatenate([t[:, :, :, :-2], t[:, :, :, 1:-1], t[:, :, :, 2:]], axis=-2)

    qb = phase(q).reshape(B, dilation, H, nb, C_BLOCK, hd)
    kb, vb = band(k), band(v)
    blk = jnp.arange(nb)[:, None]
    qpos = blk * C_BLOCK + jnp.arange(C_BLOCK)[None, :]
    kpos = (blk - 1) * C_BLOCK + jnp.arange(3 * C_BLOCK)[None, :]
    mask = ((jnp.abs(qpos[:, :, None] - kpos[:, None, :]) <= radius)
            & (kpos[:, None, :] >= 0) & (kpos[:, None, :] < L))
    s = jnp.einsum('bphnqd,bphnkd->bphnqk', qb, kb).astype(jnp.float32)
    s = jnp.where(mask, s, MASK_VALUE)
    lse = jax.nn.logsumexp(s, axis=-1, keepdims=True)
    p = jnp.exp(s - lse)
    o = jnp.einsum('bphnqk,bphnkd->bphnqd', p.astype(v.dtype), vb)
    o = o.reshape(B, dilation, H, Lp, hd)[:, :, :, :L].transpose(0, 3, 1, 2, 4).reshape(B, S, H, hd)
    lse = lse[..., 0].reshape(B, dilation, H, Lp)[..., :L].transpose(0, 3, 1, 2).reshape(B, S, H)
    return o, lse


def dilated_attention(h, w_qkv, w_o, cos, sin):
    B, S, _ = h.shape
    G = len(C_GROUPS)
    qkv = (h @ w_qkv).reshape(B, S, G, 3, C_HEADS, C_HEAD_DIM)
    q = apply_rope(qkv[:, :, :, 0], cos, sin) * (C_HEAD_DIM ** -0.5)
    k = apply_rope(qkv[:, :, :, 1], cos, sin)
    v = qkv[:, :, :, 2]
    outs, lses = [], []
    for g, (window, dilation) in enumerate(C_GROUPS):
        o_g, l_g = _dilated_group(q[:, :, g], k[:, :, g], v[:, :, g], window, dilation)
        outs.append(o_g)
        lses.append(l_g)
    wts = jax.nn.softmax(jnp.stack(lses, axis=0), axis=0)
    o = jnp.sum(wts[..., None].astype(v.dtype) * jnp.stack(outs, axis=0), axis=0)
    return o.reshape(B, S, C_HEADS * C_HEAD_DIM) @ w_o


def swiglu(h, w_gu, w_out):
    gu = h @ w_gu
    return (jax.nn.silu(gu[..., :FFN_DIM]) * gu[..., FFN_DIM:]) @ w_out


def _w(k, shape, fan_in):
    return jax.random.normal(k, shape, jnp.float32) * (fan_in ** -0.5)


def _gain(k, shape):
    return 1.0 + 0.02 * jax.random.normal(k, shape, jnp.float32)


def setup_inputs(seed: int = 0) -> dict:
    key = jax.random.key(seed)
    ks = jax.random.split(key, 21)
    D = D_MODEL
    a_qkv = 2 * 2 * A_HEADS * A_HEAD_DIM + A_HEADS * A_V_DIM
    c_qkv = len(C_GROUPS) * 3 * C_HEADS * C_HEAD_DIM
    return {
        'x': jax.random.normal(ks[0], (BATCH, SEQ, D), jnp.float32),
        'attn_norm': _gain(ks[1], (DEPTH, D)),
        'ffn_norm': _gain(ks[2], (DEPTH, D)),
        'final_norm': _gain(ks[3], (D,)),
        'a_w_qkv': _w(ks[4], (N_A, D, a_qkv), D),
        'a_lambda_q1': 0.1 * jax.random.normal(ks[5], (N_A, A_HEAD_DIM), jnp.float32),
        'a_lambda_k1': 0.1 * jax.random.normal(ks[6], (N_A, A_HEAD_DIM), jnp.float32),
        'a_lambda_q2': 0.1 * jax.random.normal(ks[7], (N_A, A_HEAD_DIM), jnp.float32),
        'a_lambda_k2': 0.1 * jax.random.normal(ks[8], (N_A, A_HEAD_DIM), jnp.float32),
        'a_subln': _gain(ks[9], (N_A, A_V_DIM)),
        'a_w_o': _w(ks[10], (N_A, A_HEADS * A_V_DIM, D), A_HEADS * A_V_DIM),
        'b_w_a': _w(ks[11], (N_B, D, B_Q_RANK + B_KV_RANK + B_ROPE), D),
        'b_q_norm': _gain(ks[12], (N_B, B_Q_RANK)),
        'b_kv_norm': _gain(ks[13], (N_B, B_KV_RANK)),
        'b_w_qb': _w(ks[14], (N_B, B_Q_RANK, B_HEADS * (B_NOPE + B_ROPE)), B_Q_RANK),
        'b_w_kvb': _w(ks[15], (N_B, B_KV_RANK, B_HEADS * (B_NOPE + B_V)), B_KV_RANK),
        'b_w_o': _w(ks[16], (N_B, B_HEADS * B_V, D), B_HEADS * B_V),
        'c_w_qkv': _w(ks[17], (N_C, D, c_qkv), D),
        'c_w_o': _w(ks[18], (N_C, C_HEADS * C_HEAD_DIM, D), C_HEADS * C_HEAD_DIM),
        'f_w_gu': _w(ks[19], (DEPTH, D, 2 * FFN_DIM), D),
        'f_w_out': _w(ks[20], (DEPTH, FFN_DIM, D), FFN_DIM),
    }


def reference(x, attn_norm, ffn_norm, final_norm, a_w_qkv, a_lambda_q1, a_lambda_k1, a_lambda_q2,
              a_lambda_k2, a_subln, a_w_o, b_w_a, b_q_norm, b_kv_norm, b_w_qb, b_w_kvb, b_w_o,
              c_w_qkv, c_w_o, f_w_gu, f_w_out):
    S = x.shape[1]
    cos_p, sin_p = rope_tables(S, A_ROT)
    cos_b, sin_b = rope_tables(S, B_ROPE)
    for i in range(DEPTH):
        m, j = i % N_MIXERS, i // N_MIXERS
        h = rms_norm(x, attn_norm[i])
        if m == 0:
            lambda_init = 0.8 - 0.6 * math.exp(-0.3 * i)
            mix = diff_attention(h, a_w_qkv[j], a_lambda_q1[j], a_lambda_k1[j], a_lambda_q2[j],
                                 a_lambda_k2[j], a_subln[j], a_w_o[j], lambda_init, cos_p, sin_p)
        elif m == 1:
            mix = latent_attention(h, b_w_a[j], b_q_norm[j], b_kv_norm[j], b_w_qb[j], b_w_kvb[j],
                                   b_w_o[j], cos_b, sin_b)
        else:
            mix = dilated_attention(h, c_w_qkv[j], c_w_o[j], cos_p, sin_p)
        x = x + mix
        x = x + swiglu(rms_norm(x, ffn_norm[i]), f_w_gu[i], f_w_out[i])
    return rms_norm(x, final_norm)
```

```python
import numpy as np
from contextlib import ExitStack
import concourse.bass as bass
import concourse.mybir as mybir
from concourse.bass_utils import run_bass_kernel_spmd

F32 = mybir.dt.float32
BF16 = mybir.dt.bfloat16
AF = mybir.ActivationFunctionType
ALU = mybir.AluOpType

S = 4096
D = 1024
NT = S // 128
FF = 2816
NJ = FF // 128
EPS = 1e-6
DEPTH = 4
ROPE_THETA = 500000.0
DBG_STAGE = 0
DBG_PERM = 0
DBG_SCR_EXT = 0


class Buf:
    __slots__ = ("name", "last_w", "readers")

    def __init__(self, name):
        self.name = name
        self.last_w = None
        self.readers = {}


class DSem:
    def __init__(self, h):
        self.h = h
        self.val = 0


class Eng:
    def __init__(self, name, eng, sem):
        self.name = name
        self.eng = eng
        self.sem = sem
        self.count = 0
        self.seen = {}


class KB:
    def __init__(self):
        self.nc = bass.Bass("TRN2", target_bir_lowering=False)
        self.es = ExitStack()
        nc = self.nc
        self.E = {}
        for name, eng in (("pe", nc.tensor), ("act", nc.scalar), ("dve", nc.vector),
                          ("pool", nc.gpsimd), ("sp", nc.sync)):
            sem = self.es.enter_context(nc.semaphore("sem_" + name))
            self.E[name] = Eng(name, eng, sem)
        self.nbuf = 0
        self.dsems = []
        self.nrot = 0
        self.ph = None
        self.phn = 0

    def phase_begin(self):
        assert self.ph is None
        self.ph = ExitStack()
        self.phn += 1

    def phase_end(self):
        self.barrier()
        self.ph.close()
        self.ph = None
        for e in self.E.values():
            if e.count > 20000:
                self.nrot += 1
                e.sem = self.es.enter_context(self.nc.semaphore("sem_%s_r%d" % (e.name, self.nrot)))
                e.count = 0

    def barrier(self):
        for e in self.E.values():
            for o in self.E.values():
                if o is e or o.count == 0:
                    continue
                if e.seen.get(o.sem.num, 0) < o.count:
                    e.eng.wait_ge(o.sem, o.count)
                    e.seen[o.sem.num] = o.count
            for d in self.dsems:
                if d.val and e.seen.get(d.h.num, 0) < d.val:
                    e.eng.wait_ge(d.h, d.val)
                    e.seen[d.h.num] = d.val

    def sb(self, name, shape, dt):
        if self.ph is not None and not DBG_PERM:
            return self.ph.enter_context(self.nc.sbuf_tensor("%s_%d" % (name, self.phn), shape, dt))
        return self.es.enter_context(self.nc.sbuf_tensor(name, shape, dt))

    def ps(self, name, shape, dt):
        return self.es.enter_context(self.nc.psum_tensor(name, shape, dt))

    def dsem(self, name):
        d = DSem(self.es.enter_context(self.nc.semaphore(name)))
        self.dsems.append(d)
        return d

    def buf(self, name="b"):
        self.nbuf += 1
        return Buf(name)

    def _deps(self, e, reads, writes):
        need = {}

        def add(tok):
            if tok is None:
                return
            sem, val = tok
            k = sem.num
            if k not in need or need[k][1] < val:
                need[k] = (sem, val)

        for b in reads:
            add(b.last_w)
        for b in writes:
            add(b.last_w)
            for r in b.readers.values():
                add(r)
        for k, (sem, val) in need.items():
            if e.seen.get(k, 0) >= val:
                continue
            if e.name == "pe" and sem is e.sem:
                continue
            e.eng.wait_ge(sem, val)
            e.seen[k] = val

    def _mark(self, tok, reads, writes):
        k = tok[0].num
        for b in reads:
            b.readers[k] = tok
        for b in writes:
            b.last_w = tok
            b.readers = {}

    def op(self, ename, fn, reads=(), writes=()):
        e = self.E[ename]
        self._deps(e, reads, writes)
        inst = fn(e.eng)
        inst.then_inc(e.sem, 1)
        e.count += 1
        tok = (e.sem, e.count)
        self._mark(tok, reads, writes)
        return tok

    def dma(self, qname, out, in_, sem, reads=(), writes=()):
        e = self.E[qname]
        self._deps(e, reads, writes)
        if sem.val and e.seen.get(sem.h.num, 0) < sem.val:
            e.eng.wait_ge(sem.h, sem.val)
            e.seen[sem.h.num] = sem.val
        inst = e.eng.dma_start(out=out, in_=in_)
        sem.val += 16
        inst.then_inc(sem.h, 16)
        tok = (sem.h, sem.val)
        self._mark(tok, reads, writes)
        return tok

    def wait_all(self, ename, bufs):
        e = self.E[ename]
        self._deps(e, bufs, ())


class Prog:
    def __init__(self, l0, l1, final_norm):
        self.l0, self.l1, self.final = l0, l1, final_norm
        kb = self.kb = KB()
        nc = self.nc = kb.nc
        dt = lambda n, s, d=F32, k="ExternalInput": nc.dram_tensor(n, s, d, kind=k).ap()
        self.x_in = dt("x", [S, D])
        self.y_out = dt("y", [S, D], F32, "ExternalOutput")
        self.gains = dt("gains", [128, 72])
        self.final_norm = dt("final_norm", [1, D])
        self.f_w_gu = dt("f_w_gu", [DEPTH, D, 2 * FF])
        self.f_w_out = dt("f_w_out", [DEPTH, FF, D])
        self.c_ident = dt("c_ident", [128, 128])
        self.wgu_s = nc.dram_tensor("wgu_s", [DEPTH, NJ, 128, 2048], BF16, kind="Internal").ap()
        self.wo_s = nc.dram_tensor("wo_s", [DEPTH, 4, 128, NJ * 256], BF16, kind="Internal").ap()
        self.gsb = [[kb.buf("gs") for _ in range(NJ)] for _ in range(DEPTH)]
        self.usb = [[kb.buf("us") for _ in range(NJ)] for _ in range(DEPTH)]
        self.wosb = [[kb.buf("wos") for _ in range(4)] for _ in range(DEPTH)]
        self.cv_sem = None
        self.cv_i = 0
        self.prepped = set()
        self.x_sb = kb.sb("x_sb", [128, NT, D], F32)
        self.xb = [kb.buf("x%d" % t) for t in range(NT)]
        self.ident = kb.sb("ident", [128, 128], BF16)
        self.identb = kb.buf("ident")
        self.csem = kb.dsem("csem")
        self.csem_p = kb.dsem("csem_p")
        kb.dma("pool", self.ident[:], self.c_ident, self.csem_p, writes=[self.identb])
        self.g_all = kb.sb("g_all", [128, 9, 8], F32)
        self.gb = kb.buf("g_all")
        kb.dma("sp", self.g_all[:].rearrange("p l f -> p (l f)"), self.gains, self.csem, writes=[self.gb])
        self.csem2 = kb.dsem("csem2")
        self.psS = kb.ps("psS", [128, 2048], F32)
        self.pb = [self.psS[:, k * 512:(k + 1) * 512] for k in range(4)]
        self.pb += [kb.ps("pb%d" % i, [128, 512], F32)[:] for i in range(4, 8)]
        self.pbb = [kb.buf("pb%d" % i) for i in range(8)]
        self.ptrs = [self.pb[6].bitcast(BF16), self.pb[7].bitcast(BF16)]
        self.ptrb = [self.pbb[6], self.pbb[7]]
        self.ss = kb.sb("ss", [128, 8], F32)
        self.ssb = kb.buf("ss")
        self.rstd = kb.sb("rstd", [128, 8], F32)
        self.rstdb = kb.buf("rstd")
        self.epsc = kb.sb("epsc", [128, 1], F32)
        self.epscb = kb.buf("epsc")
        kb.op("dve", lambda e: e.memset(self.epsc[:], EPS), writes=[self.epscb])
        self.nrm_i = 0
        self.tr_i = 0

    def alloc_norm_tmp(self):
        kb = self.kb
        self.junk = kb.sb("junk", [128, D], BF16)
        self.junkb = kb.buf("junk")
        self.xn = kb.sb("xn", [128, 2, D], BF16)
        self.xnb = [kb.buf("xn0"), kb.buf("xn1")]

    def norm_stats(self, tts):
        kb = self.kb
        n = len(tts)
        kb.op("dve", lambda e: e.memset(self.ss[:, 0:n], 0.0), writes=[self.ssb])
        for k, tt in enumerate(tts):
            kb.op("act", lambda e: e.activation(out=self.junk[:], in_=self.x_sb[:, tt, :], func=AF.Square,
                                                accum_out=self.ss[:, k:k + 1]),
                  reads=[self.xb[tt]], writes=[self.junkb, self.ssb])
        kb.op("act", lambda e: e.activation(out=self.ss[:, 0:n], in_=self.ss[:, 0:n], func=AF.Sqrt,
                                            scale=1.0 / D, bias=self.epsc[:, 0:1]),
              reads=[self.ssb, self.epscb], writes=[self.ssb])
        kb.op("dve", lambda e: e.reciprocal(out=self.rstd[:, 0:n], in_=self.ss[:, 0:n]),
              reads=[self.ssb], writes=[self.rstdb])

    def norm_transpose_chunk(self, c, gidx, hT, hTb):
        kb = self.kb
        tts = [c * 4 + tl for tl in range(4)]
        self.norm_stats(tts)
        for tl in range(4):
            tt = tts[tl]
            s = self.nrm_i % 2
            self.nrm_i += 1
            if DBG_STAGE == 10:
                continue
            kb.op("dve", lambda e: e.tensor_scalar(out=self.xn[:, s, :], in0=self.x_sb[:, tt, :],
                                                   scalar1=self.rstd[:, tl:tl + 1], scalar2=None, op0=ALU.mult),
                  reads=[self.xb[tt], self.rstdb], writes=[self.xnb[s]])
            if DBG_STAGE == 11:
                continue
            for half in range(2):
                j = self.tr_i % 2
                self.tr_i += 1
                for q in range(4):
                    fc = half * 4 + q
                    kb.op("pe", lambda e: e.transpose(self.ptrs[j][:, q * 128:(q + 1) * 128],
                                                      self.xn[:, s, fc * 128:(fc + 1) * 128], self.ident[:]),
                          reads=[self.xnb[s], self.identb], writes=[self.ptrb[j]])
                if DBG_STAGE == 12:
                    continue
                for q in range(4):
                    fc = half * 4 + q
                    kb.op("dve", lambda e: e.tensor_scalar(
                        out=hT[:, fc, tl * 128:(tl + 1) * 128],
                        in0=self.ptrs[j][:, q * 128:(q + 1) * 128],
                        scalar1=(1.0 if DBG_STAGE == 13 else self.g_all[:, gidx, fc:fc + 1]), scalar2=None, op0=ALU.mult),
                        reads=[self.ptrb[j], self.gb], writes=[hTb])

    def load_x(self):
        kb = self.kb
        self.xsem = [kb.dsem("xsem%d" % i) for i in range(4)]
        xv = self.x_in.rearrange("(tt p) d -> p tt d", p=128)
        for g in range(4):
            kb.dma("sp", self.x_sb[:, g * 8:(g + 1) * 8, :], xv[:, g * 8:(g + 1) * 8, :], self.xsem[g],
                   writes=self.xb[g * 8:(g + 1) * 8])

    def store_out(self):
        kb = self.kb
        if self.final:
            kb.phase_begin()
            self.alloc_norm_tmp()
            self.gfin = kb.sb("gfin", [128, D], F32)
            self.gfinb = kb.buf("gfin")
            kb.dma("sp", self.gfin[:], self.final_norm.broadcast_to([128, D]), self.csem2, writes=[self.gfinb])
            ot = kb.sb("ot", [128, 2, D], F32)
            otb = [kb.buf("ot0"), kb.buf("ot1")]
            osem = [kb.dsem("osem0"), kb.dsem("osem1")]
            for tt in range(NT):
                if tt % 8 == 0:
                    self.norm_stats(list(range(tt, tt + 8)))
                s = tt % 2
                ssap = self.rstd[:, tt % 8:tt % 8 + 1]
                kb.op("dve", lambda e: e.scalar_tensor_tensor(out=ot[:, s, :], in0=self.x_sb[:, tt, :], scalar=ssap,
                                                              in1=self.gfin[:], op0=ALU.mult, op1=ALU.mult),
                      reads=[self.xb[tt], self.rstdb, self.gfinb], writes=[otb[s]])
                kb.dma("sp", self.y_out[tt * 128:(tt + 1) * 128, :], ot[:, s, :], osem[s], reads=[otb[s]])
            e = kb.E["sp"]
            for s in range(2):
                e.eng.wait_ge(osem[s].h, osem[s].val)
            kb.phase_end()
        else:
            yv = self.y_out.rearrange("(tt p) d -> p tt d", p=128)
            for g in range(4):
                kb.dma("sp", yv[:, g * 8:(g + 1) * 8, :], self.x_sb[:, g * 8:(g + 1) * 8, :], self.xsem[g],
                       reads=self.xb[g * 8:(g + 1) * 8])
            for g in range(4):
                kb.E["sp"].eng.wait_ge(self.xsem[g].h, self.xsem[g].val)

    def ffn_prep(self, i):
        kb = self.kb
        if i in self.prepped:
            return
        self.prepped.add(i)
        if self.cv_sem is None:
            self.cv_sem = [kb.dsem("cvsem%d" % k) for k in range(8)]

        def sem():
            self.cv_i += 1
            return self.cv_sem[self.cv_i % 8]
        gv = self.f_w_gu[i].rearrange("(fc p) n -> p fc n", p=128)
        for j in range(NJ):
            dst = self.wgu_s[i, j].rearrange("p (fc n) -> p fc n", n=256)
            kb.dma("pool", dst[:, :, 0:128], gv[:, :, j * 128:(j + 1) * 128], sem(), writes=[self.gsb[i][j]])
            kb.dma("pool", dst[:, :, 128:256], gv[:, :, FF + j * 128:FF + (j + 1) * 128], sem(), writes=[self.usb[i][j]])
        ov = self.f_w_out[i].rearrange("(j p) n -> p j n", p=128)
        for q in range(4):
            dst = self.wo_s[i, q].rearrange("p (j n) -> p j n", n=256)
            kb.dma("pool", dst, ov[:, :, q * 256:(q + 1) * 256], sem(), writes=[self.wosb[i][q]])

    def ffn(self, i):
        kb = self.kb
        self.ffn_prep(i)
        kb.phase_begin()
        self.alloc_norm_tmp()
        if True:
            self.f_hT = kb.sb("f_hT", [128, 8, 512], BF16)
            self.f_hTb = kb.buf("f_hT")
            self.f_act = kb.sb("f_act", [128, NJ, 512], BF16)
            self.f_actb = [kb.buf("act%d" % j) for j in range(NJ)]
            self.NGU = 3
            self.f_wgu = kb.sb("f_wgu", [128, self.NGU, 8, 256], BF16)
            self.f_wgub = [kb.buf("wgu%d" % s) for s in range(self.NGU)]
            self.f_wgub2 = [kb.buf("wgu2_%d" % s) for s in range(self.NGU)]
            if not hasattr(self, "f_wgus"):
                self.f_wgus = [kb.dsem("wgus%d" % s) for s in range(self.NGU)]
                self.f_wgus2 = [kb.dsem("wgus2_%d" % s) for s in range(self.NGU)]
                self.f_wos = [kb.dsem("wos%d" % s) for s in range(2)]
            self.f_wo = kb.sb("f_wo", [128, 2, NJ, 256], BF16)
            self.f_wob = [kb.buf("wo%d" % s) for s in range(2)]
            self.f_sg = kb.sb("f_sg", [128, 2, 512], F32)
            self.f_sgb = [kb.buf("sg0"), kb.buf("sg1")]
            self.gu_i = 0
            self.wo_i = 0
        wgu = self.f_w_gu[i]
        wout = self.f_w_out[i]
        wgu_v = wgu.rearrange("(fc p) n -> p fc n", p=128)
        wout_v = wout.rearrange("(j p) n -> p j n", p=128)
        for c in range(S // 512):
            self.norm_transpose_chunk(c, 4 + i, self.f_hT, self.f_hTb)
            if DBG_STAGE in (1, 10, 11, 12, 13):
                continue
            for j in range(NJ):
                sl = self.gu_i % self.NGU
                self.gu_i += 1
                kb.dma("sp", self.f_wgu[:, sl, :, :].rearrange("p fc n -> p (fc n)"), self.wgu_s[i, j], self.f_wgus[sl],
                       reads=[self.gsb[i][j], self.usb[i][j]], writes=[self.f_wgub[sl], self.f_wgub2[sl]])
                pg = (j % 2) * 2
                pu = pg + 1
                for fc in range(8):
                    kb.op("pe", lambda e: e.matmul(self.pb[pg][:], self.f_wgu[:, sl, fc, 0:128], self.f_hT[:, fc, :],
                                                   start=(fc == 0), stop=(fc == 7)),
                          reads=[self.f_wgub[sl], self.f_hTb], writes=[self.pbb[pg]])
                for fc in range(8):
                    kb.op("pe", lambda e: e.matmul(self.pb[pu][:], self.f_wgu[:, sl, fc, 128:256], self.f_hT[:, fc, :],
                                                   start=(fc == 0), stop=(fc == 7)),
                          reads=[self.f_wgub2[sl], self.f_hTb], writes=[self.pbb[pu]])
                s2 = j % 2
                kb.op("act", lambda e: e.activation(out=self.f_sg[:, s2, :], in_=self.pb[pg][:], func=AF.Silu),
                      reads=[self.pbb[pg]], writes=[self.f_sgb[s2]])
                kb.op("dve", lambda e: e.tensor_tensor(out=self.f_act[:, j, :], in0=self.pb[pu][:],
                                                       in1=self.f_sg[:, s2, :], op=ALU.mult),
                      reads=[self.pbb[pu], self.f_sgb[s2]], writes=[self.f_actb[j]])
            if DBG_STAGE == 2:
                continue
            for q in range(4):
                sl = self.wo_i % 2
                self.wo_i += 1
                kb.dma("sp", self.f_wo[:, sl, :, :].rearrange("p j n -> p (j n)"), self.wo_s[i, q], self.f_wos[sl],
                       reads=[self.wosb[i][q]], writes=[self.f_wob[sl]])
                for tl in range(4):
                    tt = c * 4 + tl
                    po = 4 + (self.o_i % 2)
                    self.o_i += 1
                    for j in range(NJ):
                        kb.op("pe", lambda e: e.matmul(self.pb[po][:, 0:256], self.f_act[:, j, tl * 128:(tl + 1) * 128],
                                                       self.f_wo[:, sl, j, :], start=(j == 0), stop=(j == NJ - 1)),
                              reads=[self.f_actb[j], self.f_wob[sl]], writes=[self.pbb[po]])
                    kb.op("dve", lambda e: e.tensor_tensor(out=self.x_sb[:, tt, q * 256:(q + 1) * 256],
                                                           in0=self.pb[po][:, 0:256],
                                                           in1=self.x_sb[:, tt, q * 256:(q + 1) * 256], op=ALU.add),
                          reads=[self.pbb[po], self.xb[tt]], writes=[self.xb[tt]])
        kb.phase_end()

    def attn_setup(self):
        kb = self.kb
        nc = self.nc
        dt = lambda n, sh, d=F32, k="ExternalInput": nc.dram_tensor(n, sh, d, kind=k).ap()
        self.a_w_qkv = dt("a_w_qkv", [2, D, 3072])
        self.a_w_o = dt("a_w_o", [2, D, D])
        self.a_lam = dt("a_lam", [2, 4, 64])
        self.a_subln = dt("a_subln", [2, 128, 1])
        self.c_rope_p = dt("c_rope_p", [128, NT * 128])
        self.c_rope_b = dt("c_rope_b", [128, NT * 128])
        self.b_w_a = dt("b_w_a", [1, D, 672])
        self.b_w_qb = dt("b_w_qb", [1, 384, 1536])
        self.b_w_kvb = dt("b_w_kvb", [1, 256, 2048])
        self.b_w_o = dt("b_w_o", [1, D, D])
        self.b_gain = dt("b_gain", [128, 5])
        self.c_w_qkv = dt("c_w_qkv", [1, D, 9216])
        self.c_w_o = dt("c_w_o", [1, D, D])
        self.c_mask = [dt("c_mask%d" % g, [128, C_MASKS[g][0].shape[1]]) for g in range(3)]
        self.kr_sem = [kb.dsem("krsem0"), kb.dsem("krsem1")]
        self.mk_sem = [kb.dsem("mksem%d" % g) for g in range(3)]
        self.wo_sem = [kb.dsem("wosem0"), kb.dsem("wosem1")]
        self.scr = nc.dram_tensor("scr", [S, 9216], BF16, kind=("ExternalOutput" if DBG_SCR_EXT else "Internal")).ap()
        self.rsem = [kb.dsem("rsem1"), kb.dsem("rsem2")]
        self.c_sel = dt("c_sel", [128, 64])
        self.sel_f = kb.sb("sel_f", [128, 64], F32)
        self.selb = kb.buf("sel")
        kb.dma("sp", self.sel_f[:], self.c_sel, kb.dsem("selsem"), writes=[self.selb])
        self.ones_bf = kb.sb("ones_bf", [128, 128], BF16)
        self.onesb = kb.buf("ones")
        kb.op("dve", lambda e: e.memset(self.ones_bf[:], 1.0), writes=[self.onesb])
        self.osl_sem = [kb.dsem("obsem%d" % i) for i in range(3)]
        self.wsl_sem = [kb.dsem("wsem%d" % i) for i in range(3)]
        self.ld_sem = [kb.dsem("ldsem%d" % i) for i in range(24)]
        self.ld_sem2 = [kb.dsem("ldsemB%d" % i) for i in range(12)]
        self.ob_i = 0
        self.w_i = 0
        self.pp_i = 0

    def rope_alloc(self):
        kb = self.kb
        self.rt = [[kb.sb("rt%d_%d" % (k, z), [128, 128], F32) for k in range(3)] for z in range(2)]
        self.rtb = [[kb.buf("rt%d_%d" % (k, z)) for k in range(3)] for z in range(2)]
        self.rpf = [kb.sb("rpf%d" % z, [128, 512], F32) for z in range(2)]
        self.rpfb = [kb.buf("rpf%d" % z) for z in range(2)]
        self.rp_i = 0

    def rope_post(self, ps_ap, ncols, blk, off, r, cs, csb, ob, obb, psb):
        kb = self.kb
        z = self.rp_i % 2
        self.rp_i += 1
        nb = ncols // blk
        c = cs[:, 0:nb * r].rearrange("p (b k) -> p b k", k=r)
        sn = cs[:, 64:64 + nb * r].rearrange("p (b k) -> p b k", k=r)
        t1, t2, t3 = [self.rt[z][k][:, 0:nb * r].rearrange("p (b k) -> p b k", k=r) for k in range(3)]
        t1b, t2b, t3b = self.rtb[z]
        pf = self.rpf[z][:, 0:ncols]
        pfb = self.rpfb[z]
        pv = pf.rearrange("p (b k) -> p b k", k=blk)
        x1, x2 = pv[:, :, off:off + r], pv[:, :, off + r:off + 2 * r]
        kb.op("act", lambda e: e.activation(out=pf, in_=ps_ap, func=AF.Copy), reads=[psb], writes=[pfb])
        kb.op("dve", lambda e: e.tensor_tensor(out=t1, in0=x1, in1=c, op=ALU.mult), reads=[pfb, csb], writes=[t1b])
        kb.op("dve", lambda e: e.tensor_tensor(out=t2, in0=x2, in1=sn, op=ALU.mult), reads=[pfb, csb], writes=[t2b])
        kb.op("dve", lambda e: e.tensor_tensor(out=t3, in0=x1, in1=sn, op=ALU.mult), reads=[pfb, csb], writes=[t3b])
        kb.op("dve", lambda e: e.tensor_tensor(out=x1, in0=t1, in1=t2, op=ALU.subtract),
              reads=[t1b, t2b, t3b], writes=[pfb])
        kb.op("dve", lambda e: e.tensor_tensor(out=t1, in0=x2, in1=c, op=ALU.mult), reads=[pfb, csb], writes=[t1b])
        kb.op("dve", lambda e: e.tensor_tensor(out=x2, in0=t1, in1=t3, op=ALU.add),
              reads=[t1b, t3b], writes=[pfb])
        kb.op("act", lambda e: e.activation(out=ob, in_=pf, func=AF.Copy), reads=[pfb], writes=[obb])

    def proj_simple(self, i, w_dram, ngroups, roped):
        kb = self.kb
        kb.phase_begin()
        self.alloc_norm_tmp()
        hT = kb.sb("p_hT", [128, 2, 8, 512], BF16)
        hTb = [kb.buf("p_hT0"), kb.buf("p_hT1")]
        W = kb.sb("p_W", [128, 3, 8, 512], BF16)
        Wb = [kb.buf("pW%d" % k) for k in range(3)]
        ob = kb.sb("p_ob", [128, 3, 512], BF16)
        obb = [kb.buf("pob%d" % k) for k in range(3)]
        self.rope_alloc()
        wv = w_dram.rearrange("(fc p) n -> p fc n", p=128)
        self.rope_p = kb.sb("rope_p", [128, NT, 128], F32)
        self.rope_pb = kb.buf("rope_p")
        for k4 in range(2):
            kb.dma("sp", self.rope_p[:, k4 * 16:(k4 + 1) * 16, :].rearrange("p t k -> p (t k)"),
                   self.c_rope_p[:, k4 * 2048:(k4 + 1) * 2048], self.rsem[k4], writes=[self.rope_pb])
        self.norm_transpose_chunk(0, i, hT[:, 0], hTb[0])
        for c in range(S // 512):
            hc = c % 2
            for cg in range(ngroups):
                ws = self.w_i % 3
                self.w_i += 1
                kb.dma("pool", W[:, ws, :, :], wv[:, :, cg * 512:(cg + 1) * 512], self.wsl_sem[ws], writes=[Wb[ws]])
                for tl in range(4):
                    tt = c * 4 + tl
                    p = self.pp_i % 4
                    self.pp_i += 1
                    for fc in range(8):
                        kb.op("pe", lambda e: e.matmul(self.pb[p][:], hT[:, hc, fc, tl * 128:(tl + 1) * 128], W[:, ws, fc, :],
                                                       start=(fc == 0), stop=(fc == 7)),
                              reads=[hTb[hc], Wb[ws]], writes=[self.pbb[p]])
                    os_ = self.ob_i % 3
                    self.ob_i += 1
                    if roped(cg) and DBG_STAGE != 21:
                        self.rope_post(self.pb[p][:], 512, 64, 0, 8, self.rope_p[:, tt, :], self.rope_pb,
                                       ob[:, os_, :], obb[os_], self.pbb[p])
                    else:
                        kb.op("act", lambda e: e.activation(out=ob[:, os_, :], in_=self.pb[p][:], func=AF.Copy),
                              reads=[self.pbb[p]], writes=[obb[os_]])
                    kb.dma("sp", self.scr[tt * 128:(tt + 1) * 128, cg * 512:(cg + 1) * 512], ob[:, os_, :],
                           self.osl_sem[os_], reads=[obb[os_]])
                if cg == 0 and c + 1 < S // 512:
                    self.norm_transpose_chunk(c + 1, i, hT[:, 1 - hc], hTb[1 - hc])
        kb.phase_end()

    def load_T(self, dst, dstb, srcs, stg, stgb, sbase, r0=0, sv=None, pm=False):
        kb = self.kb
        if sv is None:
            sv = self.scr.rearrange("(tt p) c -> p tt c", p=128)
        ntot = sum(n for _, n in srcs)
        for g in range(NT // 4):
            if g % 2 == 0:
                k4 = g // 2
                d0 = 0
                for si, (col0, ncols) in enumerate(srcs):
                    if pm:
                        for hh in range(2):
                            sm = self.ld_sem[sbase + k4] if hh == 0 else self.ld_sem2[(4 if sbase else 0) + k4]
                            kb.dma("sp", stg[:, k4 * 8 + hh:(k4 + 1) * 8:2, d0:d0 + ncols],
                                   sv[:, k4 * 4:(k4 + 1) * 4, hh, col0:col0 + ncols], sm, writes=[stgb[hh][k4]])
                    else:
                        kb.dma("sp", stg[:, k4 * 8:(k4 + 1) * 8, d0:d0 + ncols], sv[:, k4 * 8:(k4 + 1) * 8, col0:col0 + ncols],
                               self.ld_sem[sbase + 4 * si + k4], writes=[stgb[si][k4]])
                    d0 += ncols
            j = self.tr_i % 2
            self.tr_i += 1
            for q in range(4):
                kb.op("pe", lambda e: e.transpose(self.ptrs[j][0:ntot, q * 128:(q + 1) * 128],
                                                  stg[:, g * 4 + q, 0:ntot], self.ident[:]),
                      reads=[sb_[g // 2] for sb_ in stgb] + [self.identb], writes=[self.ptrb[j]])
            kb.op("dve", lambda e: e.tensor_copy(out=dst[0:ntot, g * 512:(g + 1) * 512], in_=self.ptrs[j][0:ntot, 0:512]),
                  reads=[self.ptrb[j]], writes=[dstb[g]])

    def load_V(self, V, Vb, col0, ncols, sv=None, pm=False):
        kb = self.kb
        if sv is None:
            sv = self.scr.rearrange("(tt p) c -> p tt c", p=128)
        for k4 in range(4):
            if pm:
                for hh in range(2):
                    sm = self.ld_sem[4 + k4] if hh == 0 else self.ld_sem2[8 + k4]
                    kb.dma("sp", V[:, k4 * 8 + hh:(k4 + 1) * 8:2, 0:ncols],
                           sv[:, k4 * 4:(k4 + 1) * 4, hh, col0:col0 + ncols], sm, writes=[Vb[k4]])
            else:
                kb.dma("sp", V[:, k4 * 8:(k4 + 1) * 8, 0:ncols], sv[:, k4 * 8:(k4 + 1) * 8, col0:col0 + ncols],
                       self.ld_sem[4 + k4], writes=[Vb[k4]])

    def dense_core(self, items, scale, M, pt, ptb, V_of, zpe):
        kb = self.kb
        n = len(items)
        npair = (n + 1) // 2
        pbanks = []

        def cnt(pi):
            return min(2, n - 2 * pi)

        def score_pair(pi):
            sp = self.s_i % 2
            self.s_i += 1
            pbanks.append(sp)
            for h in range(cnt(pi)):
                kt_ap, qt_ap, bufs, vi, mask = items[2 * pi + h]
                bk = 2 * sp + h
                kb.op("pe", lambda e: e.matmul(self.pb[bk], kt_ap, qt_ap, start=True, stop=(mask is None)),
                      reads=bufs, writes=[self.pbb[bk]])
                if mask is not None:
                    m_ap, m_b = mask
                    kb.op("pe", lambda e: e.matmul(self.pb[bk], self.ident[:], m_ap, start=False, stop=True),
                          reads=[m_b, self.identb], writes=[self.pbb[bk]])

        score_pair(0)
        for pi in range(npair):
            if pi + 1 < npair:
                score_pair(pi + 1)
            sp = pbanks[pi]
            sl = self.pt_i % len(ptb)
            self.pt_i += 1
            w = 512 * cnt(pi)
            kb.op("act", lambda e: e.activation(out=pt[:, sl, 0:w], in_=self.psS[:, sp * 1024:sp * 1024 + w],
                                                func=AF.Exp, scale=scale),
                  reads=[self.pbb[2 * sp + h] for h in range(cnt(pi))], writes=[ptb[sl]])
            for h in range(cnt(pi)):
                idx = 2 * pi + h
                v_ap, v_b = V_of(items[idx][3])
                kb.op("pe", lambda e: e.matmul(self.pb[4][0:M, :], v_ap, pt[:, sl, h * 512:(h + 1) * 512],
                                               start=(idx == 0), stop=(idx == n - 1)),
                      reads=[ptb[sl], v_b], writes=[self.pbb[4]])
            if zpe:
                for h in range(cnt(pi)):
                    idx = 2 * pi + h
                    kb.op("pe", lambda e: e.matmul(self.pb[5][0:M, :], self.ones_bf[:, 0:M], pt[:, sl, h * 512:(h + 1) * 512],
                                                   start=(idx == 0), stop=(idx == n - 1)),
                          reads=[ptb[sl], self.onesb], writes=[self.pbb[5]])
            if self.pending and pi >= 1:
                self.pending.pop(0)()

    def flush_pending(self):
        while self.pending:
            self.pending.pop(0)()

    def norm_shift(self, src, srcb, dst, dstb, defer=False):
        kb = self.kb
        kb.op("dve", lambda e: e.reciprocal(out=src[64:128, :], in_=src[64:128, :]), reads=[srcb], writes=[srcb])

        def fin():
            kb.op("pe", lambda e: e.matmul(self.pb[5][0:64, :], self.sel_f[64:128, :], src[64:128, :], start=True, stop=True),
                  reads=[srcb, self.selb], writes=[self.pbb[5]])
            kb.op("dve", lambda e: e.tensor_tensor(out=dst, in0=self.pb[5][0:64, :], in1=src[0:64, :], op=ALU.mult),
                  reads=[self.pbb[5], srcb], writes=[dstb])
        if defer:
            self.pending.append(fin)
        else:
            fin()

    def wo_accum(self, on_bf, on_b, dk, Wo, Wob, qc, defer=True):
        kb = self.kb

        def step(tl, half):
            def f():
                tt = qc * 4 + tl
                p = 6 + (self.o_i % 2)
                self.o_i += 1
                kb.op("pe", lambda e: e.matmul(self.pb[p][:], on_bf[0:dk, tl * 128:(tl + 1) * 128],
                                               Wo[0:dk, half * 512:(half + 1) * 512], start=True, stop=True),
                      reads=[on_b, Wob], writes=[self.pbb[p]])
                kb.op("dve", lambda e: e.tensor_tensor(out=self.x_sb[:, tt, half * 512:(half + 1) * 512],
                                                       in0=self.pb[p][:], in1=self.x_sb[:, tt, half * 512:(half + 1) * 512],
                                                       op=ALU.add),
                      reads=[self.pbb[p], self.xb[tt]], writes=[self.xb[tt]])
            return f
        for tl in range(4):
            for half in range(2):
                if defer:
                    self.pending.append(step(tl, half))
                else:
                    step(tl, half)()

    def attn_A(self, i):
        import math
        kb = self.kb
        j = i // 3
        lambda_init = 0.8 - 0.6 * math.exp(-0.3 * i)
        self.proj_simple(i, self.a_w_qkv[j], 6, lambda cg: cg < 4)
        if DBG_STAGE in (20, 21):
            return
        kb.phase_begin()
        if self.do_ffn:
            self.ffn_prep(i)
        QT = kb.sb("a_QT", [128, S], BF16)
        KT = kb.sb("a_KT", [128, S], BF16)
        V = kb.sb("a_V", [128, NT, 128], BF16)
        stg = kb.sb("a_stg", [128, NT, 128], BF16)
        QTb = [kb.buf("QT%d" % k) for k in range(8)]
        KTb = [kb.buf("KT%d" % k) for k in range(8)]
        Vb = [kb.buf("V%d" % k) for k in range(4)]
        stgb = [[kb.buf("stg%d" % k) for k in range(4)], [kb.buf("stg2_%d" % k) for k in range(4)]]
        stgK = kb.sb("a_stgK", [128, NT, 128], BF16)
        stgKb = [[kb.buf("stgK%d" % k) for k in range(4)], [kb.buf("stgK2_%d" % k) for k in range(4)]]
        pt = kb.sb("a_pt", [128, 3, 1024], BF16)
        ptb = [kb.buf("pt%d" % k) for k in range(3)]
        Wo = kb.sb("a_Wo", [128, 2, D], BF16)
        Wob = [kb.buf("Wo0"), kb.buf("Wo1")]
        o_sb = kb.sb("a_osb", [128, 512], F32)
        z_sb = kb.sb("a_zsb", [128, 512], F32)
        rz2 = kb.sb("a_rz2", [128, 512], F32)
        o_sbb, z_sbb, rz2b = kb.buf("osb"), kb.buf("zsb"), kb.buf("rz2")
        lamt = kb.sb("a_lamt", [128, 4, 64], F32)
        lamb = kb.buf("lam")
        lt = kb.sb("a_lt", [128, 2, 64], F32)
        ls = kb.sb("a_ls", [128, 4], F32)
        ltb, lsb = kb.buf("lt"), kb.buf("ls")
        subg = kb.sb("a_subg", [128, 1], F32)
        subgb = kb.buf("subg")
        rz = kb.sb("a_rz", [128, 512], F32)
        on0 = kb.sb("a_on0", [128, 512], F32)
        on1 = kb.sb("a_on1", [128, 512], F32)
        sq = kb.sb("a_sq", [128, 512], BF16)
        onb = kb.sb("a_onb", [128, 512], BF16)
        rzb, on0b, on1b, sqb, onbb = kb.buf("rz"), kb.buf("on0"), kb.buf("on1"), kb.buf("sq"), kb.buf("onb")
        kb.dma("sp", lamt[:].rearrange("p a k -> p (a k)"),
               self.a_lam[j].rearrange("a k -> (a k)").unsqueeze(0).broadcast_to([128, 256]), self.ld_sem[9], writes=[lamb])
        kb.dma("sp", subg[:], self.a_subln[j], self.ld_sem[10], writes=[subgb])
        kb.op("dve", lambda e: e.tensor_tensor(out=lt[:, 0, :], in0=lamt[:, 0, :], in1=lamt[:, 1, :], op=ALU.mult),
              reads=[lamb], writes=[ltb])
        kb.op("dve", lambda e: e.tensor_tensor(out=lt[:, 1, :], in0=lamt[:, 2, :], in1=lamt[:, 3, :], op=ALU.mult),
              reads=[lamb], writes=[ltb])
        kb.op("dve", lambda e: e.reduce_sum(out=ls[:, 0:2], in_=lt[:], axis=mybir.AxisListType.X), reads=[ltb], writes=[lsb])
        kb.op("act", lambda e: e.activation(out=ls[:, 0:2], in_=ls[:, 0:2], func=AF.Exp), reads=[lsb], writes=[lsb])
        kb.op("dve", lambda e: e.tensor_tensor(out=ls[:, 2:3], in0=ls[:, 1:2], in1=ls[:, 0:1], op=ALU.subtract),
              reads=[lsb], writes=[lsb])
        kb.op("dve", lambda e: e.tensor_scalar(out=ls[:, 2:3], in0=ls[:, 2:3], scalar1=-lambda_init, scalar2=None, op0=ALU.add),
              reads=[lsb], writes=[lsb])
        kb.op("dve", lambda e: e.tensor_scalar(out=subg[:], in0=subg[:], scalar1=(1.0 - lambda_init), scalar2=None, op0=ALU.mult),
              reads=[subgb], writes=[subgb])
        neg_lam = ls[:, 2:3]
        sv = self.scr.rearrange("(tt p) c -> p tt c", p=128)
        wo_d = self.a_w_o[j]
        for h in range(8):
            self.load_T(QT, QTb, [(h * 128, 128)], stg, stgb, 0)
            self.load_T(KT, KTb, [(1024 + h * 128, 128)], stgK, stgKb, 16)
            self.load_V(V, Vb, 2048 + h * 128, 128)
            kb.dma("pool", Wo[:, h % 2, :], wo_d[h * 128:(h + 1) * 128, :], self.wo_sem[h % 2], writes=[Wob[h % 2]])
            for qc in range(8):
                for m in range(2):
                    r0 = m * 64
                    items = [(KT[r0:r0 + 64, kt * 128:(kt + 1) * 128], QT[r0:r0 + 64, qc * 512:(qc + 1) * 512],
                              [KTb[kt // 4], QTb[qc]], kt, None) for kt in range(NT)]
                    self.dense_core(items, 0.125, 128, pt, ptb, lambda kt: (V[:, kt, :], Vb[kt // 8]), True)
                    self.flush_pending()
                    kb.op("act", lambda e: e.activation(out=o_sb[:], in_=self.pb[4][:], func=AF.Copy),
                          reads=[self.pbb[4]], writes=[o_sbb])
                    kb.op("act", lambda e: e.activation(out=z_sb[:], in_=self.pb[5][:], func=AF.Copy),
                          reads=[self.pbb[5]], writes=[z_sbb])
                    kb.op("dve", lambda e: e.reciprocal(out=rz[:], in_=z_sb[:]), reads=[z_sbb], writes=[rzb])
                    dst, dstb = (on0, on0b) if m == 0 else (on1, on1b)
                    kb.op("dve", lambda e: e.tensor_tensor(out=dst[:], in0=o_sb[:], in1=rz[:], op=ALU.mult),
                          reads=[o_sbb, rzb], writes=[dstb])
                kb.op("dve", lambda e: e.scalar_tensor_tensor(out=on0[:], in0=on1[:], scalar=neg_lam, in1=on0[:],
                                                              op0=ALU.mult, op1=ALU.add),
                      reads=[on1b, on0b, lsb], writes=[on0b])
                kb.op("dve", lambda e: e.tensor_tensor(out=sq[:], in0=on0[:], in1=on0[:], op=ALU.mult), reads=[on0b], writes=[sqb])

                def subln():
                    p = 6 + (self.o_i % 2)
                    self.o_i += 1
                    kb.op("pe", lambda e: e.matmul(self.pb[p][:], self.ones_bf[:], sq[:], start=True, stop=True),
                          reads=[sqb, self.onesb], writes=[self.pbb[p]])
                    kb.op("act", lambda e: e.activation(out=rz2[:], in_=self.pb[p][:], func=AF.Sqrt, scale=1.0 / 128,
                                                        bias=self.epsc[:, 0:1]),
                          reads=[self.pbb[p], self.epscb], writes=[rz2b])
                    kb.op("dve", lambda e: e.reciprocal(out=rz2[:], in_=rz2[:]), reads=[rz2b], writes=[rz2b])
                    kb.op("dve", lambda e: e.scalar_tensor_tensor(out=onb[:], in0=on0[:], scalar=subg[:, 0:1], in1=rz2[:],
                                                                  op0=ALU.mult, op1=ALU.mult),
                          reads=[on0b, subgb, rz2b], writes=[onbb])
                self.pending.append(subln)
                self.wo_accum(onb, onbb, 128, Wo[:, h % 2, :], Wob[h % 2], qc)
        self.flush_pending()
        kb.phase_end()

    def attn_B(self, i):
        kb = self.kb
        j = i // 3
        KV0, KR0 = 1536, 3584
        kb.phase_begin()
        self.alloc_norm_tmp()
        hT = kb.sb("b_hT", [128, 8, 512], BF16)
        hTb = kb.buf("b_hT")
        Wa = kb.sb("b_Wa", [128, 8, 672], BF16)
        Wqb = kb.sb("b_Wqb", [128, 3, 1536], BF16)
        Wkvb = kb.sb("b_Wkvb", [128, 2, 2048], BF16)
        Wab, Wqbb, Wkvbb = kb.buf("Wa"), kb.buf("Wqb"), kb.buf("Wkvb")
        lT = kb.sb("b_lT", [128, 5, 512], BF16)
        lTb = [kb.buf("lT%d" % k) for k in range(4)]
        af = kb.sb("b_af", [128, 672], F32)
        afb = kb.buf("af")
        latn = kb.sb("b_latn", [128, 640], BF16)
        latnb = kb.buf("latn")
        st = kb.sb("b_st", [128, 4], F32)
        stb = kb.buf("st")
        bg = kb.sb("b_bg", [128, 5], F32)
        bgb = kb.buf("bg")
        ob = kb.sb("b_ob", [128, 3, 512], BF16)
        obb = [kb.buf("bob%d" % k) for k in range(3)]
        okr = kb.sb("b_okr", [128, 2, 32], BF16)
        okrb = [kb.buf("okr0"), kb.buf("okr1")]
        self.rope_alloc()
        rope_b = kb.sb("rope_b", [128, NT, 128], F32)
        rope_bb = kb.buf("rope_b")
        for k4 in range(2):
            kb.dma("sp", rope_b[:, k4 * 16:(k4 + 1) * 16, :].rearrange("p t k -> p (t k)"),
                   self.c_rope_b[:, k4 * 2048:(k4 + 1) * 2048], self.rsem[k4], writes=[rope_bb])
        kb.dma("sp", bg[:], self.b_gain, self.ld_sem[9], writes=[bgb])
        kb.dma("pool", Wa[:], self.b_w_a[j].rearrange("(fc p) n -> p fc n", p=128), self.wsl_sem[0], writes=[Wab])
        kb.dma("pool", Wqb[:], self.b_w_qb[j].rearrange("(kc p) n -> p kc n", p=128), self.wsl_sem[1], writes=[Wqbb])
        kb.dma("pool", Wkvb[:], self.b_w_kvb[j].rearrange("(kc p) n -> p kc n", p=128), self.wsl_sem[2], writes=[Wkvbb])
        kr_i = 0
        for c in range(S // 512):
            self.norm_transpose_chunk(c, i, hT, hTb)
            for tl in range(4):
                tt = c * 4 + tl
                p1 = self.pp_i % 4
                p2 = (self.pp_i + 1) % 4
                self.pp_i += 2
                for fc in range(8):
                    kb.op("pe", lambda e: e.matmul(self.pb[p1][:], hT[:, fc, tl * 128:(tl + 1) * 128], Wa[:, fc, 0:512],
                                                   start=(fc == 0), stop=(fc == 7)),
                          reads=[hTb, Wab], writes=[self.pbb[p1]])
                for fc in range(8):
                    kb.op("pe", lambda e: e.matmul(self.pb[p2][:, 0:160], hT[:, fc, tl * 128:(tl + 1) * 128],
                                                   Wa[:, fc, 512:672], start=(fc == 0), stop=(fc == 7)),
                          reads=[hTb, Wab], writes=[self.pbb[p2]])
                kb.op("act", lambda e: e.activation(out=af[:, 0:512], in_=self.pb[p1][:], func=AF.Copy),
                      reads=[self.pbb[p1]], writes=[afb])
                kb.op("act", lambda e: e.activation(out=af[:, 512:672], in_=self.pb[p2][:, 0:160], func=AF.Copy),
                      reads=[self.pbb[p2]], writes=[afb])
                kb.op("dve", lambda e: e.memset(st[:, 0:2], 0.0), writes=[stb])
                kb.op("act", lambda e: e.activation(out=self.junk[:, 0:384], in_=af[:, 0:384], func=AF.Square,
                                                    accum_out=st[:, 0:1]), reads=[afb], writes=[self.junkb, stb])
                kb.op("act", lambda e: e.activation(out=self.junk[:, 0:256], in_=af[:, 384:640], func=AF.Square,
                                                    accum_out=st[:, 1:2]), reads=[afb], writes=[self.junkb, stb])
                kb.op("act", lambda e: e.activation(out=st[:, 0:1], in_=st[:, 0:1], func=AF.Sqrt, scale=1.0 / 384,
                                                    bias=self.epsc[:, 0:1]), reads=[stb, self.epscb], writes=[stb])
                kb.op("act", lambda e: e.activation(out=st[:, 1:2], in_=st[:, 1:2], func=AF.Sqrt, scale=1.0 / 256,
                                                    bias=self.epsc[:, 0:1]), reads=[stb, self.epscb], writes=[stb])
                kb.op("dve", lambda e: e.reciprocal(out=st[:, 2:4], in_=st[:, 0:2]), reads=[stb], writes=[stb])
                kb.op("dve", lambda e: e.tensor_scalar(out=latn[:, 0:384], in0=af[:, 0:384], scalar1=st[:, 2:3],
                                                       scalar2=None, op0=ALU.mult), reads=[afb, stb], writes=[latnb])
                kb.op("dve", lambda e: e.tensor_scalar(out=latn[:, 384:640], in0=af[:, 384:640], scalar1=st[:, 3:4],
                                                       scalar2=None, op0=ALU.mult), reads=[afb, stb], writes=[latnb])
                ks = kr_i % 2
                kr_i += 1
                self.rope_post(af[:, 640:672], 32, 32, 0, 16, rope_b[:, tt, :], rope_bb, okr[:, ks, :], okrb[ks], afb)
                kb.dma("sp", self.scr[tt * 128:(tt + 1) * 128, KR0:KR0 + 32], okr[:, ks, :], self.kr_sem[ks],
                       reads=[okrb[ks]])
                jj = self.tr_i % 2
                self.tr_i += 1
                for kc in range(5):
                    kb.op("pe", lambda e: e.transpose(self.ptrs[jj][:, kc * 128:(kc + 1) * 128],
                                                      latn[:, kc * 128:(kc + 1) * 128], self.ident[:]),
                          reads=[latnb, self.identb], writes=[self.ptrb[jj]])
                for kc in range(5):
                    kb.op("dve", lambda e: e.tensor_scalar(out=lT[:, kc, tl * 128:(tl + 1) * 128],
                                                           in0=self.ptrs[jj][:, kc * 128:(kc + 1) * 128],
                                                           scalar1=bg[:, kc:kc + 1], scalar2=None, op0=ALU.mult),
                          reads=[self.ptrb[jj], bgb], writes=[lTb[tl]])
                for g4 in range(4):
                    p = self.pp_i % 4
                    self.pp_i += 1
                    for kc in range(3):
                        kb.op("pe", lambda e: e.matmul(self.pb[p][:, 0:384], lT[:, kc, tl * 128:(tl + 1) * 128],
                                                       Wqb[:, kc, g4 * 384:(g4 + 1) * 384], start=(kc == 0), stop=(kc == 2)),
                              reads=[lTb[tl], Wqbb], writes=[self.pbb[p]])
                    os_ = self.ob_i % 3
                    self.ob_i += 1
                    self.rope_post(self.pb[p][:, 0:384], 384, 96, 64, 16, rope_b[:, tt, :], rope_bb,
                                   ob[:, os_, 0:384], obb[os_], self.pbb[p])
                    kb.dma("sp", self.scr[tt * 128:(tt + 1) * 128, g4 * 384:(g4 + 1) * 384], ob[:, os_, 0:384],
                           self.osl_sem[os_], reads=[obb[os_]])
                for g4 in range(4):
                    p = self.pp_i % 4
                    self.pp_i += 1
                    for kc in range(2):
                        kb.op("pe", lambda e: e.matmul(self.pb[p][:], lT[:, 3 + kc, tl * 128:(tl + 1) * 128],
                                                       Wkvb[:, kc, g4 * 512:(g4 + 1) * 512], start=(kc == 0), stop=(kc == 1)),
                              reads=[lTb[tl], Wkvbb], writes=[self.pbb[p]])
                    os_ = self.ob_i % 3
                    self.ob_i += 1
                    kb.op("act", lambda e: e.activation(out=ob[:, os_, :], in_=self.pb[p][:], func=AF.Copy),
                          reads=[self.pbb[p]], writes=[obb[os_]])
                    kb.dma("sp", self.scr[tt * 128:(tt + 1) * 128, KV0 + g4 * 512:KV0 + (g4 + 1) * 512], ob[:, os_, :],
                           self.osl_sem[os_], reads=[obb[os_]])
        kb.phase_end()
        kb.phase_begin()
        if self.do_ffn:
            self.ffn_prep(i)
        QT = kb.sb("b_QT", [128, S], BF16)
        KT = kb.sb("b_KT", [128, S], BF16)
        V = kb.sb("b_V", [128, NT, 128], BF16)
        stg = kb.sb("b_stg", [128, NT, 96], BF16)
        QTb = [kb.buf("QT%d" % k) for k in range(8)]
        KTb = [kb.buf("KT%d" % k) for k in range(8)]
        Vb = [kb.buf("V%d" % k) for k in range(4)]
        stgb = [[kb.buf("stg%d" % k) for k in range(4)], [kb.buf("stg2_%d" % k) for k in range(4)]]
        stgK = kb.sb("b_stgK", [128, NT, 96], BF16)
        stgKb = [[kb.buf("stgK%d" % k) for k in range(4)], [kb.buf("stgK2_%d" % k) for k in range(4)]]
        pt = kb.sb("b_pt", [128, 3, 1024], BF16)
        ptb = [kb.buf("pt%d" % k) for k in range(3)]
        Wo = kb.sb("b_Wo", [64, 2, D], BF16)
        Wob = [kb.buf("Wo0"), kb.buf("Wo1")]
        oz = kb.sb("b_oz", [128, 512], F32)
        onb = kb.sb("b_onb", [64, 512], BF16)
        ozb, onbb = kb.buf("oz"), kb.buf("onb")
        kb.op("dve", lambda e: e.memset(V[:, :, 64:128], 1.0), writes=Vb)
        wo_d = self.b_w_o[j]
        scale = 96.0 ** -0.5
        for h in range(16):
            self.load_T(QT, QTb, [(h * 96, 96)], stg, stgb, 0)
            self.load_T(KT, KTb, [(KV0 + h * 128, 64), (KR0, 32)], stgK, stgKb, 16)
            self.load_V(V, Vb, KV0 + h * 128 + 64, 64)
            kb.dma("pool", Wo[:, h % 2, :], wo_d[h * 64:(h + 1) * 64, :], self.wo_sem[h % 2], writes=[Wob[h % 2]])
            for qc in range(8):
                items = [(KT[0:96, kt * 128:(kt + 1) * 128], QT[0:96, qc * 512:(qc + 1) * 512],
                          [KTb[kt // 4], QTb[qc]], kt, None) for kt in range(NT)]
                self.dense_core(items, scale, 128, pt, ptb, lambda kt: (V[:, kt, :], Vb[kt // 8]), False)
                self.flush_pending()
                kb.op("act", lambda e: e.activation(out=oz[:], in_=self.pb[4][:], func=AF.Copy), reads=[self.pbb[4]], writes=[ozb])
                self.norm_shift(oz, ozb, onb[:], onbb, defer=True)
                self.wo_accum(onb, onbb, 64, Wo[:, h % 2, :], Wob[h % 2], qc)
        self.flush_pending()
        kb.phase_end()

    def attn_C(self, i):
        kb = self.kb
        j = i // 3
        self.proj_simple(i, self.c_w_qkv[j], 18, lambda cg: (cg % 6) < 4)
        kb.phase_begin()
        if self.do_ffn:
            self.ffn_prep(i)
        QT = kb.sb("c_QT", [128, S], BF16)
        KT = kb.sb("c_KT", [128, S], BF16)
        V = kb.sb("c_V", [128, NT, 128], BF16)
        stg = kb.sb("c_stg", [128, NT, 64], BF16)
        QTb = [kb.buf("QT%d" % k) for k in range(8)]
        KTb = [kb.buf("KT%d" % k) for k in range(8)]
        Vb = [kb.buf("V%d" % k) for k in range(4)]
        stgb = [[kb.buf("stg%d" % k) for k in range(4)], [kb.buf("stg2_%d" % k) for k in range(4)]]
        stgK = kb.sb("c_stgK", [128, NT, 64], BF16)
        stgKb = [[kb.buf("stgK%d" % k) for k in range(4)], [kb.buf("stgK2_%d" % k) for k in range(4)]]
        pt = kb.sb("c_pt", [128, 3, 1024], BF16)
        ptb = [kb.buf("pt%d" % k) for k in range(3)]
        Wo = kb.sb("c_Wo", [64, 2, D], BF16)
        Wob = [kb.buf("Wo0"), kb.buf("Wo1")]
        onb_all = kb.sb("c_onball", [64, S], BF16)
        onball_b = [kb.buf("onball%d" % q) for q in range(8)]
        ctmp = kb.sb("c_tmp", [128, 512], F32)
        ctmpb = kb.buf("ctmp")
        acc = kb.sb("c_acc", [128, S], F32)
        accb = [kb.buf("acc%d" % q) for q in range(8)]
        kb.op("dve", lambda e: e.memset(V[:, :, 64:128], 1.0), writes=Vb)
        mk = [kb.sb("c_mk%d" % g, [128, C_MASKS[g][0].shape[1]], BF16) for g in range(3)]
        mkb = [kb.buf("mk%d" % g) for g in range(3)]
        for g in range(3):
            kb.dma("pool", mk[g][:], self.c_mask[g], self.mk_sem[g], writes=[mkb[g]])
        wo_d = self.c_w_o[j]
        moff = [0]
        for m in C_MASKS:
            moff.append(moff[-1] + len(m))
        for h in range(16):
            kb.dma("pool", Wo[:, h % 2, :], wo_d[h * 64:(h + 1) * 64, :], self.wo_sem[h % 2], writes=[Wob[h % 2]])
            for g in range(3):
                base = g * 3072 + h * 64
                svg = self.scr.rearrange("(h p ph) c -> p ph h c", h=2, p=128, ph=16) if g == 2 else None
                self.load_T(QT, QTb, [(base, 64)], stg, stgb, 0, sv=svg, pm=(g == 2))
                self.load_T(KT, KTb, [(base + 1024, 64)], stgK, stgKb, 16, sv=svg, pm=(g == 2))
                self.load_V(V, Vb, base + 2048, 64, sv=svg, pm=(g == 2))
                for qc in range(8):
                    items = []
                    for kt in (range(NT) if g < 2 else range(4 * qc, 4 * qc + 4)):
                        o = kt * 128 - qc * 512
                        mi = C_MIDX[g].get(o) if g < 2 else C_MIDX[2][kt - 4 * qc]
                        if mi is None:
                            continue
                        items.append((KT[0:64, kt * 128:(kt + 1) * 128], QT[0:64, qc * 512:(qc + 1) * 512],
                                      [KTb[kt // 4], QTb[qc]], kt, (mk[g][:, mi:mi + 512], mkb[g])))
                    self.dense_core(items, 0.125, 128, pt, ptb, lambda kt: (V[:, kt, :], Vb[kt // 8]), False)
                    sl = slice(qc * 512, (qc + 1) * 512)
                    if g == 0:
                        kb.op("act", lambda e: e.activation(out=acc[:, sl], in_=self.pb[4][:], func=AF.Copy),
                              reads=[self.pbb[4]], writes=[accb[qc]])
                    elif g == 2:
                        kb.op("act", lambda e: e.activation(out=ctmp[:], in_=self.pb[4][:], func=AF.Copy),
                              reads=[self.pbb[4]], writes=[ctmpb])
                        av = acc[:].rearrange("p (m ph) -> p m ph", ph=16)[:, :, 2 * qc:2 * qc + 2]
                        tv = ctmp[:].rearrange("p (ph m) -> p m ph", ph=2)
                        kb.op("dve", lambda e: e.tensor_tensor(out=av, in0=tv, in1=av, op=ALU.add),
                              reads=[ctmpb] + accb, writes=accb)
                    else:
                        kb.op("dve", lambda e: e.tensor_tensor(out=acc[:, sl], in0=self.pb[4][:], in1=acc[:, sl],
                                                               op=ALU.add), reads=[self.pbb[4], accb[qc]], writes=[accb[qc]])
            self.flush_pending()
            for qc in range(8):
                sl = slice(qc * 512, (qc + 1) * 512)
                self.norm_shift(acc[:, sl], accb[qc], onb_all[:, sl], onball_b[qc])
                self.wo_accum(onb_all[:, sl], onball_b[qc], 64, Wo[:, h % 2, :], Wob[h % 2], qc)
        self.flush_pending()
        kb.phase_end()

    def build(self, do_attn=True, do_ffn=True):
        self.do_ffn = do_ffn
        self.load_x()
        self.s_i = 0
        self.pt_i = 0
        self.o_i = 0
        self.pending = []
        if do_attn:
            self.attn_setup()
        for i in range(self.l0, self.l1):
            if do_attn:
                getattr(self, 'attn_' + 'ABC'[i % 3])(i)
            if do_ffn:
                self.ffn(i)
        self.store_out()
        return self.nc


def _gains(attn_norm, ffn_norm, final_norm):
    allg = np.concatenate([attn_norm, ffn_norm, final_norm.reshape(1, D)], axis=0)
    return np.ascontiguousarray(allg.reshape(9, 8, 128).transpose(2, 0, 1).reshape(128, 72)).astype(np.float32)


def _build_c_masks():
    groups = ((128, 1), (512, 4), (2048, 16))
    i = np.arange(128)[:, None]
    jq = np.arange(512)[None, :]
    masks, midx = [], []
    for (w, d) in groups:
        R = w // 2
        uniq, keys, idx = [], {}, {}
        for o in range(-2048, 2049, 128):
            delta = jq - i - o
            cond = (np.abs(delta) <= R) & (delta % d == 0)
            if not cond.any():
                continue
            m = np.where(cond, 0.0, -30000.0).astype(np.float32)
            kbytes = m.tobytes()
            if kbytes not in keys:
                keys[kbytes] = len(uniq)
                uniq.append(m)
            idx[o] = keys[kbytes]
        omin, omax = min(idx), max(idx)
        wdt = 512 + omax - omin
        cc = np.arange(wdt)[None, :]
        dl = cc - omax - i
        base = np.where((np.abs(dl) <= R) & (dl % d == 0), 0.0, -30000.0).astype(np.float32)
        for o in idx:
            assert np.array_equal(base[:, omax - o:omax - o + 512], uniq[idx[o]])
        masks.append([base])
        midx.append({o: omax - o for o in idx})
    jj = np.arange(512)[None, :]
    ii = np.arange(128)[:, None]
    uniq, idx = [], {}
    for r in range(4):
        cond = ((jj // 256) == (r // 2)) & (np.abs((jj % 256) - ((r % 2) * 128 + ii)) <= 64)
        uniq.append(np.where(cond, 0.0, -30000.0).astype(np.float32))
        idx[r] = r
    masks[2] = [np.concatenate(uniq, axis=1)]
    midx[2] = {r: 512 * r for r in range(4)}
    return masks, midx


C_MASKS, C_MIDX = _build_c_masks()


def _rope_tok(rot):
    pos = np.arange(S, dtype=np.float32)
    inv = (ROPE_THETA ** (-np.arange(0, rot, 2, dtype=np.float32) / rot)).astype(np.float32)
    ang = pos[:, None] * inv[None, :]
    rep = 64 // (rot // 2)
    t = np.concatenate([np.tile(np.cos(ang), (1, rep)), np.tile(np.sin(ang), (1, rep))], axis=1).astype(np.float32)
    return np.ascontiguousarray(t.reshape(NT, 128, 128).transpose(1, 0, 2).reshape(128, NT * 128))


def _consts():
    sel = np.zeros((128, 64), np.float32)
    sel[64 + np.arange(64), np.arange(64)] = 1.0
    return {"c_sel": sel, "c_ident": np.eye(128, dtype=np.float32), "c_rope_p": _rope_tok(16), "c_rope_b": _rope_tok(32)}


def _layout_inputs(inp):
    d = {"gains": _gains(inp["attn_norm"], inp["ffn_norm"], inp["final_norm"]),
         "final_norm": np.ascontiguousarray(inp["final_norm"].reshape(1, D)),
         "f_w_gu": inp["f_w_gu"], "f_w_out": inp["f_w_out"],
         "a_w_qkv": inp["a_w_qkv"], "a_w_o": inp["a_w_o"],
         "a_lam": np.ascontiguousarray(np.stack([inp["a_lambda_q1"], inp["a_lambda_k1"], inp["a_lambda_q2"],
                                                 inp["a_lambda_k2"]], axis=1)),
         "a_subln": np.ascontiguousarray(inp["a_subln"].reshape(2, 128, 1)),
         "b_w_a": inp["b_w_a"], "b_w_qb": inp["b_w_qb"], "b_w_kvb": inp["b_w_kvb"], "b_w_o": inp["b_w_o"],
         "b_gain": np.ascontiguousarray(np.concatenate([inp["b_q_norm"][0], inp["b_kv_norm"][0]]).reshape(5, 128).T),
         "c_w_qkv": inp["c_w_qkv"], "c_w_o": inp["c_w_o"],
         "c_mask0": C_MASKS[0][0], "c_mask1": C_MASKS[1][0], "c_mask2": C_MASKS[2][0]}
    d.update(_consts())
    return d


def kernel(**inputs):
    inp = {k: np.asarray(v, dtype=np.float32) for k, v in inputs.items()}
    n = inp["x"].shape[0]
    nc = Prog(0, DEPTH, True).build()
    shared = _layout_inputs(inp)
    in_maps = []
    for b in range(n):
        m = dict(shared)
        m["x"] = np.ascontiguousarray(inp["x"][b])
        in_maps.append(m)
    res = run_bass_kernel_spmd(nc, in_maps, core_ids=list(range(n)))
    return np.stack([np.asarray(r["y"], dtype=np.float32) for r in res.results], axis=0)
```

```python
import numpy as np
from contextlib import ExitStack
import concourse.bass as bass
import concourse.mybir as mybir
from concourse.bass_utils import run_bass_kernel_spmd

F32 = mybir.dt.float32
BF16 = mybir.dt.bfloat16
AF = mybir.ActivationFunctionType
ALU = mybir.AluOpType

S = 4096
D = 1024
NT = S // 128
FF = 2816
NJ = FF // 128
EPS = 1e-6
DEPTH = 4
ROPE_THETA = 500000.0
DBG_STAGE = 0
DBG_PERM = 0
DBG_SCR_EXT = 0


class Buf:
    __slots__ = ("name", "last_w", "readers")

    def __init__(self, name):
        self.name = name
        self.last_w = None
        self.readers = {}


class DSem:
    def __init__(self, h):
        self.h = h
        self.val = 0


class Eng:
    def __init__(self, name, eng, sem):
        self.name = name
        self.eng = eng
        self.sem = sem
        self.count = 0
        self.seen = {}


class KB:
    def __init__(self):
        self.nc = bass.Bass("TRN2", target_bir_lowering=False)
        self.es = ExitStack()
        nc = self.nc
        self.E = {}
        for name, eng in (("pe", nc.tensor), ("act", nc.scalar), ("dve", nc.vector),
                          ("pool", nc.gpsimd), ("sp", nc.sync)):
            sem = self.es.enter_context(nc.semaphore("sem_" + name))
            self.E[name] = Eng(name, eng, sem)
        self.nbuf = 0
        self.dsems = []
        self.nrot = 0
        self.ph = None
        self.phn = 0

    def phase_begin(self):
        assert self.ph is None
        self.ph = ExitStack()
        self.phn += 1

    def phase_end(self):
        self.barrier()
        self.ph.close()
        self.ph = None
        for e in self.E.values():
            if e.count > 20000:
                self.nrot += 1
                e.sem = self.es.enter_context(self.nc.semaphore("sem_%s_r%d" % (e.name, self.nrot)))
                e.count = 0

    def barrier(self):
        for e in self.E.values():
            for o in self.E.values():
                if o is e or o.count == 0:
                    continue
                if e.seen.get(o.sem.num, 0) < o.count:
                    e.eng.wait_ge(o.sem, o.count)
                    e.seen[o.sem.num] = o.count
            for d in self.dsems:
                if d.val and e.seen.get(d.h.num, 0) < d.val:
                    e.eng.wait_ge(d.h, d.val)
                    e.seen[d.h.num] = d.val

    def sb(self, name, shape, dt):
        if self.ph is not None and not DBG_PERM:
            return self.ph.enter_context(self.nc.sbuf_tensor("%s_%d" % (name, self.phn), shape, dt))
        return self.es.enter_context(self.nc.sbuf_tensor(name, shape, dt))

    def ps(self, name, shape, dt):
        return self.es.enter_context(self.nc.psum_tensor(name, shape, dt))

    def dsem(self, name):
        d = DSem(self.es.enter_context(self.nc.semaphore(name)))
        self.dsems.append(d)
        return d

    def buf(self, name="b"):
        self.nbuf += 1
        return Buf(name)

    def _deps(self, e, reads, writes):
        need = {}

        def add(tok):
            if tok is None:
                return
            sem, val = tok
            k = sem.num
            if k not in need or need[k][1] < val:
                need[k] = (sem, val)

        for b in reads:
            add(b.last_w)
        for b in writes:
            add(b.last_w)
            for r in b.readers.values():
                add(r)
        for k, (sem, val) in need.items():
            if e.seen.get(k, 0) >= val:
                continue
            if e.name == "pe" and sem is e.sem:
                continue
            e.eng.wait_ge(sem, val)
            e.seen[k] = val

    def _mark(self, tok, reads, writes):
        k = tok[0].num
        for b in reads:
            b.readers[k] = tok
        for b in writes:
            b.last_w = tok
            b.readers = {}

    def op(self, ename, fn, reads=(), writes=()):
        e = self.E[ename]
        self._deps(e, reads, writes)
        inst = fn(e.eng)
        inst.then_inc(e.sem, 1)
        e.count += 1
        tok = (e.sem, e.count)
        self._mark(tok, reads, writes)
        return tok

    def dma(self, qname, out, in_, sem, reads=(), writes=()):
        e = self.E[qname]
        self._deps(e, reads, writes)
        if sem.val and e.seen.get(sem.h.num, 0) < sem.val:
            e.eng.wait_ge(sem.h, sem.val)
            e.seen[sem.h.num] = sem.val
        inst = e.eng.dma_start(out=out, in_=in_)
        sem.val += 16
        inst.then_inc(sem.h, 16)
        tok = (sem.h, sem.val)
        self._mark(tok, reads, writes)
        return tok

    def wait_all(self, ename, bufs):
        e = self.E[ename]
        self._deps(e, bufs, ())


class Prog:
    def __init__(self, l0, l1, final_norm):
        self.l0, self.l1, self.final = l0, l1, final_norm
        kb = self.kb = KB()
        nc = self.nc = kb.nc
        dt = lambda n, s, d=F32, k="ExternalInput": nc.dram_tensor(n, s, d, kind=k).ap()
        self.x_in = dt("x", [S, D])
        self.y_out = dt("y", [S, D], F32, "ExternalOutput")
        self.gains = dt("gains", [128, 72])
        self.final_norm = dt("final_norm", [1, D])
        self.f_w_gu = dt("f_w_gu", [DEPTH, D, 2 * FF])
        self.f_w_out = dt("f_w_out", [DEPTH, FF, D])
        self.c_ident = dt("c_ident", [128, 128])
        self.wgu_s = nc.dram_tensor("wgu_s", [DEPTH, NJ, 128, 2048], BF16, kind="Internal").ap()
        self.wo_s = nc.dram_tensor("wo_s", [DEPTH, 4, 128, NJ * 256], BF16, kind="Internal").ap()
        self.gsb = [[kb.buf("gs") for _ in range(NJ)] for _ in range(DEPTH)]
        self.usb = [[kb.buf("us") for _ in range(NJ)] for _ in range(DEPTH)]
        self.wosb = [[kb.buf("wos") for _ in range(4)] for _ in range(DEPTH)]
        self.cv_sem = None
        self.cv_i = 0
        self.prepped = set()
        self.x_sb = kb.sb("x_sb", [128, NT, D], F32)
        self.xb = [kb.buf("x%d" % t) for t in range(NT)]
        self.ident = kb.sb("ident", [128, 128], BF16)
        self.identb = kb.buf("ident")
        self.csem = kb.dsem("csem")
        self.csem_p = kb.dsem("csem_p")
        kb.dma("pool", self.ident[:], self.c_ident, self.csem_p, writes=[self.identb])
        self.g_all = kb.sb("g_all", [128, 9, 8], F32)
        self.gb = kb.buf("g_all")
        kb.dma("sp", self.g_all[:].rearrange("p l f -> p (l f)"), self.gains, self.csem, writes=[self.gb])
        self.csem2 = kb.dsem("csem2")
        self.psS = kb.ps("psS", [128, 2048], F32)
        self.pb = [self.psS[:, k * 512:(k + 1) * 512] for k in range(4)]
        self.pb += [kb.ps("pb%d" % i, [128, 512], F32)[:] for i in range(4, 8)]
        self.pbb = [kb.buf("pb%d" % i) for i in range(8)]
        self.ptrs = [self.pb[6].bitcast(BF16), self.pb[7].bitcast(BF16)]
        self.ptrb = [self.pbb[6], self.pbb[7]]
        self.ss = kb.sb("ss", [128, 8], F32)
        self.ssb = kb.buf("ss")
        self.rstd = kb.sb("rstd", [128, 8], F32)
        self.rstdb = kb.buf("rstd")
        self.epsc = kb.sb("epsc", [128, 1], F32)
        self.epscb = kb.buf("epsc")
        kb.op("dve", lambda e: e.memset(self.epsc[:], EPS), writes=[self.epscb])
        self.nrm_i = 0
        self.tr_i = 0

    def alloc_norm_tmp(self):
        kb = self.kb
        self.junk = kb.sb("junk", [128, D], BF16)
        self.junkb = kb.buf("junk")
        self.xn = kb.sb("xn", [128, 2, D], BF16)
        self.xnb = [kb.buf("xn0"), kb.buf("xn1")]

    def norm_stats(self, tts):
        kb = self.kb
        n = len(tts)
        kb.op("dve", lambda e: e.memset(self.ss[:, 0:n], 0.0), writes=[self.ssb])
        for k, tt in enumerate(tts):
            kb.op("act", lambda e: e.activation(out=self.junk[:], in_=self.x_sb[:, tt, :], func=AF.Square,
                                                accum_out=self.ss[:, k:k + 1]),
                  reads=[self.xb[tt]], writes=[self.junkb, self.ssb])
        kb.op("act", lambda e: e.activation(out=self.ss[:, 0:n], in_=self.ss[:, 0:n], func=AF.Sqrt,
                                            scale=1.0 / D, bias=self.epsc[:, 0:1]),
              reads=[self.ssb, self.epscb], writes=[self.ssb])
        kb.op("dve", lambda e: e.reciprocal(out=self.rstd[:, 0:n], in_=self.ss[:, 0:n]),
              reads=[self.ssb], writes=[self.rstdb])

    def norm_transpose_chunk(self, c, gidx, hT, hTb):
        kb = self.kb
        tts = [c * 4 + tl for tl in range(4)]
        self.norm_stats(tts)
        for tl in range(4):
            tt = tts[tl]
            s = self.nrm_i % 2
            self.nrm_i += 1
            if DBG_STAGE == 10:
                continue
            kb.op("dve", lambda e: e.tensor_scalar(out=self.xn[:, s, :], in0=self.x_sb[:, tt, :],
                                                   scalar1=self.rstd[:, tl:tl + 1], scalar2=None, op0=ALU.mult),
                  reads=[self.xb[tt], self.rstdb], writes=[self.xnb[s]])
            if DBG_STAGE == 11:
                continue
            for half in range(2):
                j = self.tr_i % 2
                self.tr_i += 1
                for q in range(4):
                    fc = half * 4 + q
                    kb.op("pe", lambda e: e.transpose(self.ptrs[j][:, q * 128:(q + 1) * 128],
                                                      self.xn[:, s, fc * 128:(fc + 1) * 128], self.ident[:]),
                          reads=[self.xnb[s], self.identb], writes=[self.ptrb[j]])
                if DBG_STAGE == 12:
                    continue
                for q in range(4):
                    fc = half * 4 + q
                    kb.op("dve", lambda e: e.tensor_scalar(
                        out=hT[:, fc, tl * 128:(tl + 1) * 128],
                        in0=self.ptrs[j][:, q * 128:(q + 1) * 128],
                        scalar1=(1.0 if DBG_STAGE == 13 else self.g_all[:, gidx, fc:fc + 1]), scalar2=None, op0=ALU.mult),
                        reads=[self.ptrb[j], self.gb], writes=[hTb])

    def load_x(self):
        kb = self.kb
        self.xsem = [kb.dsem("xsem%d" % i) for i in range(4)]
        xv = self.x_in.rearrange("(tt p) d -> p tt d", p=128)
        for g in range(4):
            kb.dma("sp", self.x_sb[:, g * 8:(g + 1) * 8, :], xv[:, g * 8:(g + 1) * 8, :], self.xsem[g],
                   writes=self.xb[g * 8:(g + 1) * 8])

    def store_out(self):
        kb = self.kb
        if self.final:
            kb.phase_begin()
            self.alloc_norm_tmp()
            self.gfin = kb.sb("gfin", [128, D], F32)
            self.gfinb = kb.buf("gfin")
            kb.dma("sp", self.gfin[:], self.final_norm.broadcast_to([128, D]), self.csem2, writes=[self.gfinb])
            ot = kb.sb("ot", [128, 2, D], F32)
            otb = [kb.buf("ot0"), kb.buf("ot1")]
            osem = [kb.dsem("osem0"), kb.dsem("osem1")]
            for tt in range(NT):
                if tt % 8 == 0:
                    self.norm_stats(list(range(tt, tt + 8)))
                s = tt % 2
                ssap = self.rstd[:, tt % 8:tt % 8 + 1]
                kb.op("dve", lambda e: e.scalar_tensor_tensor(out=ot[:, s, :], in0=self.x_sb[:, tt, :], scalar=ssap,
                                                              in1=self.gfin[:], op0=ALU.mult, op1=ALU.mult),
                      reads=[self.xb[tt], self.rstdb, self.gfinb], writes=[otb[s]])
                kb.dma("sp", self.y_out[tt * 128:(tt + 1) * 128, :], ot[:, s, :], osem[s], reads=[otb[s]])
            e = kb.E["sp"]
            for s in range(2):
                e.eng.wait_ge(osem[s].h, osem[s].val)
            kb.phase_end()
        else:
            yv = self.y_out.rearrange("(tt p) d -> p tt d", p=128)
            for g in range(4):
                kb.dma("sp", yv[:, g * 8:(g + 1) * 8, :], self.x_sb[:, g * 8:(g + 1) * 8, :], self.xsem[g],
                       reads=self.xb[g * 8:(g + 1) * 8])
            for g in range(4):
                kb.E["sp"].eng.wait_ge(self.xsem[g].h, self.xsem[g].val)

    def ffn_prep(self, i):
        kb = self.kb
        if i in self.prepped:
            return
        self.prepped.add(i)
        if self.cv_sem is None:
            self.cv_sem = [kb.dsem("cvsem%d" % k) for k in range(8)]

        def sem():
            self.cv_i += 1
            return self.cv_sem[self.cv_i % 8]
        gv = self.f_w_gu[i].rearrange("(fc p) n -> p fc n", p=128)
        for j in range(NJ):
            dst = self.wgu_s[i, j].rearrange("p (fc n) -> p fc n", n=256)
            kb.dma("pool", dst[:, :, 0:128], gv[:, :, j * 128:(j + 1) * 128], sem(), writes=[self.gsb[i][j]])
            kb.dma("pool", dst[:, :, 128:256], gv[:, :, FF + j * 128:FF + (j + 1) * 128], sem(), writes=[self.usb[i][j]])
        ov = self.f_w_out[i].rearrange("(j p) n -> p j n", p=128)
        for q in range(4):
            dst = self.wo_s[i, q].rearrange("p (j n) -> p j n", n=256)
            kb.dma("pool", dst, ov[:, :, q * 256:(q + 1) * 256], sem(), writes=[self.wosb[i][q]])

    def ffn(self, i):
        kb = self.kb
        self.ffn_prep(i)
        kb.phase_begin()
        self.alloc_norm_tmp()
        if True:
            self.f_hT = kb.sb("f_hT", [128, 8, 512], BF16)
            self.f_hTb = kb.buf("f_hT")
            self.f_act = kb.sb("f_act", [128, NJ, 512], BF16)
            self.f_actb = [kb.buf("act%d" % j) for j in range(NJ)]
            self.NGU = 3
            self.f_wgu = kb.sb("f_wgu", [128, self.NGU, 8, 256], BF16)
            self.f_wgub = [kb.buf("wgu%d" % s) for s in range(self.NGU)]
            self.f_wgub2 = [kb.buf("wgu2_%d" % s) for s in range(self.NGU)]
            if not hasattr(self, "f_wgus"):
                self.f_wgus = [kb.dsem("wgus%d" % s) for s in range(self.NGU)]
                self.f_wgus2 = [kb.dsem("wgus2_%d" % s) for s in range(self.NGU)]
                self.f_wos = [kb.dsem("wos%d" % s) for s in range(2)]
            self.f_wo = kb.sb("f_wo", [128, 2, NJ, 256], BF16)
            self.f_wob = [kb.buf("wo%d" % s) for s in range(2)]
            self.f_sg = kb.sb("f_sg", [128, 2, 512], F32)
            self.f_sgb = [kb.buf("sg0"), kb.buf("sg1")]
            self.gu_i = 0
            self.wo_i = 0
        wgu = self.f_w_gu[i]
        wout = self.f_w_out[i]
        wgu_v = wgu.rearrange("(fc p) n -> p fc n", p=128)
        wout_v = wout.rearrange("(j p) n -> p j n", p=128)
        for c in range(S // 512):
            self.norm_transpose_chunk(c, 4 + i, self.f_hT, self.f_hTb)
            if DBG_STAGE in (1, 10, 11, 12, 13):
                continue
            for j in range(NJ):
                sl = self.gu_i % self.NGU
                self.gu_i += 1
                kb.dma("sp", self.f_wgu[:, sl, :, :].rearrange("p fc n -> p (fc n)"), self.wgu_s[i, j], self.f_wgus[sl],
                       reads=[self.gsb[i][j], self.usb[i][j]], writes=[self.f_wgub[sl], self.f_wgub2[sl]])
                pg = (j % 2) * 2
                pu = pg + 1
                for fc in range(8):
                    kb.op("pe", lambda e: e.matmul(self.pb[pg][:], self.f_wgu[:, sl, fc, 0:128], self.f_hT[:, fc, :],
                                                   start=(fc == 0), stop=(fc == 7)),
                          reads=[self.f_wgub[sl], self.f_hTb], writes=[self.pbb[pg]])
                for fc in range(8):
                    kb.op("pe", lambda e: e.matmul(self.pb[pu][:], self.f_wgu[:, sl, fc, 128:256], self.f_hT[:, fc, :],
                                                   start=(fc == 0), stop=(fc == 7)),
                          reads=[self.f_wgub2[sl], self.f_hTb], writes=[self.pbb[pu]])
                s2 = j % 2
                kb.op("act", lambda e: e.activation(out=self.f_sg[:, s2, :], in_=self.pb[pg][:], func=AF.Silu),
                      reads=[self.pbb[pg]], writes=[self.f_sgb[s2]])
                kb.op("dve", lambda e: e.tensor_tensor(out=self.f_act[:, j, :], in0=self.pb[pu][:],
                                                       in1=self.f_sg[:, s2, :], op=ALU.mult),
                      reads=[self.pbb[pu], self.f_sgb[s2]], writes=[self.f_actb[j]])
            if DBG_STAGE == 2:
                continue
            for q in range(4):
                sl = self.wo_i % 2
                self.wo_i += 1
                kb.dma("sp", self.f_wo[:, sl, :, :].rearrange("p j n -> p (j n)"), self.wo_s[i, q], self.f_wos[sl],
                       reads=[self.wosb[i][q]], writes=[self.f_wob[sl]])
                for tl in range(4):
                    tt = c * 4 + tl
                    po = 4 + (self.o_i % 2)
                    self.o_i += 1
                    for j in range(NJ):
                        kb.op("pe", lambda e: e.matmul(self.pb[po][:, 0:256], self.f_act[:, j, tl * 128:(tl + 1) * 128],
                                                       self.f_wo[:, sl, j, :], start=(j == 0), stop=(j == NJ - 1)),
                              reads=[self.f_actb[j], self.f_wob[sl]], writes=[self.pbb[po]])
                    kb.op("dve", lambda e: e.tensor_tensor(out=self.x_sb[:, tt, q * 256:(q + 1) * 256],
                                                           in0=self.pb[po][:, 0:256],
                                                           in1=self.x_sb[:, tt, q * 256:(q + 1) * 256], op=ALU.add),
                          reads=[self.pbb[po], self.xb[tt]], writes=[self.xb[tt]])
        kb.phase_end()

    def attn_setup(self):
        kb = self.kb
        nc = self.nc
        dt = lambda n, sh, d=F32, k="ExternalInput": nc.dram_tensor(n, sh, d, kind=k).ap()
        self.a_w_qkv = dt("a_w_qkv", [2, D, 3072])
        self.a_w_o = dt("a_w_o", [2, D, D])
        self.a_lam = dt("a_lam", [2, 4, 64])
        self.a_subln = dt("a_subln", [2, 128, 1])
        self.c_rope_p = dt("c_rope_p", [128, NT * 128])
        self.c_rope_b = dt("c_rope_b", [128, NT * 128])
        self.b_w_a = dt("b_w_a", [1, D, 672])
        self.b_w_qb = dt("b_w_qb", [1, 384, 1536])
        self.b_w_kvb = dt("b_w_kvb", [1, 256, 2048])
        self.b_w_o = dt("b_w_o", [1, D, D])
        self.b_gain = dt("b_gain", [128, 5])
        self.c_w_qkv = dt("c_w_qkv", [1, D, 9216])
        self.c_w_o = dt("c_w_o", [1, D, D])
        self.c_mask = [dt("c_mask%d" % g, [128, C_MASKS[g][0].shape[1]]) for g in range(3)]
        self.kr_sem = [kb.dsem("krsem0"), kb.dsem("krsem1")]
        self.mk_sem = [kb.dsem("mksem%d" % g) for g in range(3)]
        self.wo_sem = [kb.dsem("wosem0"), kb.dsem("wosem1")]
        self.scr = nc.dram_tensor("scr", [S, 9216], BF16, kind=("ExternalOutput" if DBG_SCR_EXT else "Internal")).ap()
        self.rsem = [kb.dsem("rsem1"), kb.dsem("rsem2")]
        self.c_sel = dt("c_sel", [128, 64])
        self.sel_f = kb.sb("sel_f", [128, 64], F32)
        self.selb = kb.buf("sel")
        kb.dma("sp", self.sel_f[:], self.c_sel, kb.dsem("selsem"), writes=[self.selb])
        self.ones_bf = kb.sb("ones_bf", [128, 128], BF16)
        self.onesb = kb.buf("ones")
        kb.op("dve", lambda e: e.memset(self.ones_bf[:], 1.0), writes=[self.onesb])
        self.osl_sem = [kb.dsem("obsem%d" % i) for i in range(3)]
        self.wsl_sem = [kb.dsem("wsem%d" % i) for i in range(3)]
        self.ld_sem = [kb.dsem("ldsem%d" % i) for i in range(24)]
        self.ld_sem2 = [kb.dsem("ldsemB%d" % i) for i in range(12)]
        self.ob_i = 0
        self.w_i = 0
        self.pp_i = 0

    def rope_alloc(self):
        kb = self.kb
        self.rt = [[kb.sb("rt%d_%d" % (k, z), [128, 128], F32) for k in range(3)] for z in range(2)]
        self.rtb = [[kb.buf("rt%d_%d" % (k, z)) for k in range(3)] for z in range(2)]
        self.rpf = [kb.sb("rpf%d" % z, [128, 512], F32) for z in range(2)]
        self.rpfb = [kb.buf("rpf%d" % z) for z in range(2)]
        self.rp_i = 0

    def rope_post(self, ps_ap, ncols, blk, off, r, cs, csb, ob, obb, psb):
        kb = self.kb
        z = self.rp_i % 2
        self.rp_i += 1
        nb = ncols // blk
        c = cs[:, 0:nb * r].rearrange("p (b k) -> p b k", k=r)
        sn = cs[:, 64:64 + nb * r].rearrange("p (b k) -> p b k", k=r)
        t1, t2, t3 = [self.rt[z][k][:, 0:nb * r].rearrange("p (b k) -> p b k", k=r) for k in range(3)]
        t1b, t2b, t3b = self.rtb[z]
        pf = self.rpf[z][:, 0:ncols]
        pfb = self.rpfb[z]
        pv = pf.rearrange("p (b k) -> p b k", k=blk)
        x1, x2 = pv[:, :, off:off + r], pv[:, :, off + r:off + 2 * r]
        kb.op("act", lambda e: e.activation(out=pf, in_=ps_ap, func=AF.Copy), reads=[psb], writes=[pfb])
        kb.op("dve", lambda e: e.tensor_tensor(out=t1, in0=x1, in1=c, op=ALU.mult), reads=[pfb, csb], writes=[t1b])
        kb.op("dve", lambda e: e.tensor_tensor(out=t2, in0=x2, in1=sn, op=ALU.mult), reads=[pfb, csb], writes=[t2b])
        kb.op("dve", lambda e: e.tensor_tensor(out=t3, in0=x1, in1=sn, op=ALU.mult), reads=[pfb, csb], writes=[t3b])
        kb.op("dve", lambda e: e.tensor_tensor(out=x1, in0=t1, in1=t2, op=ALU.subtract),
              reads=[t1b, t2b, t3b], writes=[pfb])
        kb.op("dve", lambda e: e.tensor_tensor(out=t1, in0=x2, in1=c, op=ALU.mult), reads=[pfb, csb], writes=[t1b])
        kb.op("dve", lambda e: e.tensor_tensor(out=x2, in0=t1, in1=t3, op=ALU.add),
              reads=[t1b, t3b], writes=[pfb])
        kb.op("act", lambda e: e.activation(out=ob, in_=pf, func=AF.Copy), reads=[pfb], writes=[obb])

    def proj_simple(self, i, w_dram, ngroups, roped):
        kb = self.kb
        kb.phase_begin()
        self.alloc_norm_tmp()
        hT = kb.sb("p_hT", [128, 2, 8, 512], BF16)
        hTb = [kb.buf("p_hT0"), kb.buf("p_hT1")]
        W = kb.sb("p_W", [128, 3, 8, 512], BF16)
        Wb = [kb.buf("pW%d" % k) for k in range(3)]
        ob = kb.sb("p_ob", [128, 3, 512], BF16)
        obb = [kb.buf("pob%d" % k) for k in range(3)]
        self.rope_alloc()
        wv = w_dram.rearrange("(fc p) n -> p fc n", p=128)
        self.rope_p = kb.sb("rope_p", [128, NT, 128], F32)
        self.rope_pb = kb.buf("rope_p")
        for k4 in range(2):
            kb.dma("sp", self.rope_p[:, k4 * 16:(k4 + 1) * 16, :].rearrange("p t k -> p (t k)"),
                   self.c_rope_p[:, k4 * 2048:(k4 + 1) * 2048], self.rsem[k4], writes=[self.rope_pb])
        self.norm_transpose_chunk(0, i, hT[:, 0], hTb[0])
        for c in range(S // 512):
            hc = c % 2
            for cg in range(ngroups):
                ws = self.w_i % 3
                self.w_i += 1
                kb.dma("pool", W[:, ws, :, :], wv[:, :, cg * 512:(cg + 1) * 512], self.wsl_sem[ws], writes=[Wb[ws]])
                for tl in range(4):
                    tt = c * 4 + tl
                    p = self.pp_i % 4
                    self.pp_i += 1
                    for fc in range(8):
                        kb.op("pe", lambda e: e.matmul(self.pb[p][:], hT[:, hc, fc, tl * 128:(tl + 1) * 128], W[:, ws, fc, :],
                                                       start=(fc == 0), stop=(fc == 7)),
                              reads=[hTb[hc], Wb[ws]], writes=[self.pbb[p]])
                    os_ = self.ob_i % 3
                    self.ob_i += 1
                    if roped(cg) and DBG_STAGE != 21:
                        self.rope_post(self.pb[p][:], 512, 64, 0, 8, self.rope_p[:, tt, :], self.rope_pb,
                                       ob[:, os_, :], obb[os_], self.pbb[p])
                    else:
                        kb.op("act", lambda e: e.activation(out=ob[:, os_, :], in_=self.pb[p][:], func=AF.Copy),
                              reads=[self.pbb[p]], writes=[obb[os_]])
                    kb.dma("sp", self.scr[tt * 128:(tt + 1) * 128, cg * 512:(cg + 1) * 512], ob[:, os_, :],
                           self.osl_sem[os_], reads=[obb[os_]])
                if cg == 0 and c + 1 < S // 512:
                    self.norm_transpose_chunk(c + 1, i, hT[:, 1 - hc], hTb[1 - hc])
        kb.phase_end()

    def load_T(self, dst, dstb, srcs, stg, stgb, sbase, r0=0, sv=None, pm=False):
        kb = self.kb
        if sv is None:
            sv = self.scr.rearrange("(tt p) c -> p tt c", p=128)
        ntot = sum(n for _, n in srcs)
        for g in range(NT // 4):
            if g % 2 == 0:
                k4 = g // 2
                d0 = 0
                for si, (col0, ncols) in enumerate(srcs):
                    if pm:
                        for hh in range(2):
                            sm = self.ld_sem[sbase + k4] if hh == 0 else self.ld_sem2[(4 if sbase else 0) + k4]
                            kb.dma("sp", stg[:, k4 * 8 + hh:(k4 + 1) * 8:2, d0:d0 + ncols],
                                   sv[:, k4 * 4:(k4 + 1) * 4, hh, col0:col0 + ncols], sm, writes=[stgb[hh][k4]])
                    else:
                        kb.dma("sp", stg[:, k4 * 8:(k4 + 1) * 8, d0:d0 + ncols], sv[:, k4 * 8:(k4 + 1) * 8, col0:col0 + ncols],
                               self.ld_sem[sbase + 4 * si + k4], writes=[stgb[si][k4]])
                    d0 += ncols
            j = self.tr_i % 2
            self.tr_i += 1
            for q in range(4):
                kb.op("pe", lambda e: e.transpose(self.ptrs[j][0:ntot, q * 128:(q + 1) * 128],
                                                  stg[:, g * 4 + q, 0:ntot], self.ident[:]),
                      reads=[sb_[g // 2] for sb_ in stgb] + [self.identb], writes=[self.ptrb[j]])
            kb.op("dve", lambda e: e.tensor_copy(out=dst[0:ntot, g * 512:(g + 1) * 512], in_=self.ptrs[j][0:ntot, 0:512]),
                  reads=[self.ptrb[j]], writes=[dstb[g]])

    def load_V(self, V, Vb, col0, ncols, sv=None, pm=False):
        kb = self.kb
        if sv is None:
            sv = self.scr.rearrange("(tt p) c -> p tt c", p=128)
        for k4 in range(4):
            if pm:
                for hh in range(2):
                    sm = self.ld_sem[4 + k4] if hh == 0 else self.ld_sem2[8 + k4]
                    kb.dma("sp", V[:, k4 * 8 + hh:(k4 + 1) * 8:2, 0:ncols],
                           sv[:, k4 * 4:(k4 + 1) * 4, hh, col0:col0 + ncols], sm, writes=[Vb[k4]])
            else:
                kb.dma("sp", V[:, k4 * 8:(k4 + 1) * 8, 0:ncols], sv[:, k4 * 8:(k4 + 1) * 8, col0:col0 + ncols],
                       self.ld_sem[4 + k4], writes=[Vb[k4]])

    def dense_core(self, items, scale, M, pt, ptb, V_of, zpe):
        kb = self.kb
        n = len(items)
        npair = (n + 1) // 2
        pbanks = []

        def cnt(pi):
            return min(2, n - 2 * pi)

        def score_pair(pi):
            sp = self.s_i % 2
            self.s_i += 1
            pbanks.append(sp)
            for h in range(cnt(pi)):
                kt_ap, qt_ap, bufs, vi, mask = items[2 * pi + h]
                bk = 2 * sp + h
                kb.op("pe", lambda e: e.matmul(self.pb[bk], kt_ap, qt_ap, start=True, stop=(mask is None)),
                      reads=bufs, writes=[self.pbb[bk]])
                if mask is not None:
                    m_ap, m_b = mask
                    kb.op("pe", lambda e: e.matmul(self.pb[bk], self.ident[:], m_ap, start=False, stop=True),
                          reads=[m_b, self.identb], writes=[self.pbb[bk]])

        score_pair(0)
        for pi in range(npair):
            if pi + 1 < npair:
                score_pair(pi + 1)
            sp = pbanks[pi]
            sl = self.pt_i % len(ptb)
            self.pt_i += 1
            w = 512 * cnt(pi)
            kb.op("act", lambda e: e.activation(out=pt[:, sl, 0:w], in_=self.psS[:, sp * 1024:sp * 1024 + w],
                                                func=AF.Exp, scale=scale),
                  reads=[self.pbb[2 * sp + h] for h in range(cnt(pi))], writes=[ptb[sl]])
            for h in range(cnt(pi)):
                idx = 2 * pi + h
                v_ap, v_b = V_of(items[idx][3])
                kb.op("pe", lambda e: e.matmul(self.pb[4][0:M, :], v_ap, pt[:, sl, h * 512:(h + 1) * 512],
                                               start=(idx == 0), stop=(idx == n - 1)),
                      reads=[ptb[sl], v_b], writes=[self.pbb[4]])
            if zpe:
                for h in range(cnt(pi)):
                    idx = 2 * pi + h
                    kb.op("pe", lambda e: e.matmul(self.pb[5][0:M, :], self.ones_bf[:, 0:M], pt[:, sl, h * 512:(h + 1) * 512],
                                                   start=(idx == 0), stop=(idx == n - 1)),
                          reads=[ptb[sl], self.onesb], writes=[self.pbb[5]])
            if self.pending and pi >= 1:
                self.pending.pop(0)()

    def flush_pending(self):
        while self.pending:
            self.pending.pop(0)()

    def norm_shift(self, src, srcb, dst, dstb, defer=False):
        kb = self.kb
        kb.op("dve", lambda e: e.reciprocal(out=src[64:128, :], in_=src[64:128, :]), reads=[srcb], writes=[srcb])

        def fin():
            kb.op("pe", lambda e: e.matmul(self.pb[5][0:64, :], self.sel_f[64:128, :], src[64:128, :], start=True, stop=True),
                  reads=[srcb, self.selb], writes=[self.pbb[5]])
            kb.op("dve", lambda e: e.tensor_tensor(out=dst, in0=self.pb[5][0:64, :], in1=src[0:64, :], op=ALU.mult),
                  reads=[self.pbb[5], srcb], writes=[dstb])
        if defer:
            self.pending.append(fin)
        else:
            fin()

    def wo_accum(self, on_bf, on_b, dk, Wo, Wob, qc, defer=True):
        kb = self.kb

        def step(tl, half):
            def f():
                tt = qc * 4 + tl
                p = 6 + (self.o_i % 2)
                self.o_i += 1
                kb.op("pe", lambda e: e.matmul(self.pb[p][:], on_bf[0:dk, tl * 128:(tl + 1) * 128],
                                               Wo[0:dk, half * 512:(half + 1) * 512], start=True, stop=True),
                      reads=[on_b, Wob], writes=[self.pbb[p]])
                kb.op("dve", lambda e: e.tensor_tensor(out=self.x_sb[:, tt, half * 512:(half + 1) * 512],
                                                       in0=self.pb[p][:], in1=self.x_sb[:, tt, half * 512:(half + 1) * 512],
                                                       op=ALU.add),
                      reads=[self.pbb[p], self.xb[tt]], writes=[self.xb[tt]])
            return f
        for tl in range(4):
            for half in range(2):
                if defer:
                    self.pending.append(step(tl, half))
                else:
                    step(tl, half)()

    def attn_A(self, i):
        import math
        kb = self.kb
        j = i // 3
        lambda_init = 0.8 - 0.6 * math.exp(-0.3 * i)
        self.proj_simple(i, self.a_w_qkv[j], 6, lambda cg: cg < 4)
        if DBG_STAGE in (20, 21):
            return
        kb.phase_begin()
        if self.do_ffn:
            self.ffn_prep(i)
        QT = kb.sb("a_QT", [128, S], BF16)
        KT = kb.sb("a_KT", [128, S], BF16)
        V = kb.sb("a_V", [128, NT, 128], BF16)
        stg = kb.sb("a_stg", [128, NT, 128], BF16)
        QTb = [kb.buf("QT%d" % k) for k in range(8)]
        KTb = [kb.buf("KT%d" % k) for k in range(8)]
        Vb = [kb.buf("V%d" % k) for k in range(4)]
        stgb = [[kb.buf("stg%d" % k) for k in range(4)], [kb.buf("stg2_%d" % k) for k in range(4)]]
        stgK = kb.sb("a_stgK", [128, NT, 128], BF16)
        stgKb = [[kb.buf("stgK%d" % k) for k in range(4)], [kb.buf("stgK2_%d" % k) for k in range(4)]]
        pt = kb.sb("a_pt", [128, 3, 1024], BF16)
        ptb = [kb.buf("pt%d" % k) for k in range(3)]
        Wo = kb.sb("a_Wo", [128, 2, D], BF16)
        Wob = [kb.buf("Wo0"), kb.buf("Wo1")]
        o_sb = kb.sb("a_osb", [128, 512], F32)
        z_sb = kb.sb("a_zsb", [128, 512], F32)
        rz2 = kb.sb("a_rz2", [128, 512], F32)
        o_sbb, z_sbb, rz2b = kb.buf("osb"), kb.buf("zsb"), kb.buf("rz2")
        lamt = kb.sb("a_lamt", [128, 4, 64], F32)
        lamb = kb.buf("lam")
        lt = kb.sb("a_lt", [128, 2, 64], F32)
        ls = kb.sb("a_ls", [128, 4], F32)
        ltb, lsb = kb.buf("lt"), kb.buf("ls")
        subg = kb.sb("a_subg", [128, 1], F32)
        subgb = kb.buf("subg")
        rz = kb.sb("a_rz", [128, 512], F32)
        on0 = kb.sb("a_on0", [128, 512], F32)
        on1 = kb.sb("a_on1", [128, 512], F32)
        sq = kb.sb("a_sq", [128, 512], BF16)
        onb = kb.sb("a_onb", [128, 512], BF16)
        rzb, on0b, on1b, sqb, onbb = kb.buf("rz"), kb.buf("on0"), kb.buf("on1"), kb.buf("sq"), kb.buf("onb")
        kb.dma("sp", lamt[:].rearrange("p a k -> p (a k)"),
               self.a_lam[j].rearrange("a k -> (a k)").unsqueeze(0).broadcast_to([128, 256]), self.ld_sem[9], writes=[lamb])
        kb.dma("sp", subg[:], self.a_subln[j], self.ld_sem[10], writes=[subgb])
        kb.op("dve", lambda e: e.tensor_tensor(out=lt[:, 0, :], in0=lamt[:, 0, :], in1=lamt[:, 1, :], op=ALU.mult),
              reads=[lamb], writes=[ltb])
        kb.op("dve", lambda e: e.tensor_tensor(out=lt[:, 1, :], in0=lamt[:, 2, :], in1=lamt[:, 3, :], op=ALU.mult),
              reads=[lamb], writes=[ltb])
        kb.op("dve", lambda e: e.reduce_sum(out=ls[:, 0:2], in_=lt[:], axis=mybir.AxisListType.X), reads=[ltb], writes=[lsb])
        kb.op("act", lambda e: e.activation(out=ls[:, 0:2], in_=ls[:, 0:2], func=AF.Exp), reads=[lsb], writes=[lsb])
        kb.op("dve", lambda e: e.tensor_tensor(out=ls[:, 2:3], in0=ls[:, 1:2], in1=ls[:, 0:1], op=ALU.subtract),
              reads=[lsb], writes=[lsb])
        kb.op("dve", lambda e: e.tensor_scalar(out=ls[:, 2:3], in0=ls[:, 2:3], scalar1=-lambda_init, scalar2=None, op0=ALU.add),
              reads=[lsb], writes=[lsb])
        kb.op("dve", lambda e: e.tensor_scalar(out=subg[:], in0=subg[:], scalar1=(1.0 - lambda_init), scalar2=None, op0=ALU.mult),
              reads=[subgb], writes=[subgb])
        neg_lam = ls[:, 2:3]
        sv = self.scr.rearrange("(tt p) c -> p tt c", p=128)
        wo_d = self.a_w_o[j]
        for h in range(8):
            self.load_T(QT, QTb, [(h * 128, 128)], stg, stgb, 0)
            self.load_T(KT, KTb, [(1024 + h * 128, 128)], stgK, stgKb, 16)
            self.load_V(V, Vb, 2048 + h * 128, 128)
            kb.dma("pool", Wo[:, h % 2, :], wo_d[h * 128:(h + 1) * 128, :], self.wo_sem[h % 2], writes=[Wob[h % 2]])
            for qc in range(8):
                for m in range(2):
                    r0 = m * 64
                    items = [(KT[r0:r0 + 64, kt * 128:(kt + 1) * 128], QT[r0:r0 + 64, qc * 512:(qc + 1) * 512],
                              [KTb[kt // 4], QTb[qc]], kt, None) for kt in range(NT)]
                    self.dense_core(items, 0.125, 128, pt, ptb, lambda kt: (V[:, kt, :], Vb[kt // 8]), True)
                    self.flush_pending()
                    kb.op("act", lambda e: e.activation(out=o_sb[:], in_=self.pb[4][:], func=AF.Copy),
                          reads=[self.pbb[4]], writes=[o_sbb])
                    kb.op("act", lambda e: e.activation(out=z_sb[:], in_=self.pb[5][:], func=AF.Copy),
                          reads=[self.pbb[5]], writes=[z_sbb])
                    kb.op("dve", lambda e: e.reciprocal(out=rz[:], in_=z_sb[:]), reads=[z_sbb], writes=[rzb])
                    dst, dstb = (on0, on0b) if m == 0 else (on1, on1b)
                    kb.op("dve", lambda e: e.tensor_tensor(out=dst[:], in0=o_sb[:], in1=rz[:], op=ALU.mult),
                          reads=[o_sbb, rzb], writes=[dstb])
                kb.op("dve", lambda e: e.scalar_tensor_tensor(out=on0[:], in0=on1[:], scalar=neg_lam, in1=on0[:],
                                                              op0=ALU.mult, op1=ALU.add),
                      reads=[on1b, on0b, lsb], writes=[on0b])
                kb.op("dve", lambda e: e.tensor_tensor(out=sq[:], in0=on0[:], in1=on0[:], op=ALU.mult), reads=[on0b], writes=[sqb])

                def subln():
                    p = 6 + (self.o_i % 2)
                    self.o_i += 1
                    kb.op("pe", lambda e: e.matmul(self.pb[p][:], self.ones_bf[:], sq[:], start=True, stop=True),
                          reads=[sqb, self.onesb], writes=[self.pbb[p]])
                    kb.op("act", lambda e: e.activation(out=rz2[:], in_=self.pb[p][:], func=AF.Sqrt, scale=1.0 / 128,
                                                        bias=self.epsc[:, 0:1]),
                          reads=[self.pbb[p], self.epscb], writes=[rz2b])
                    kb.op("dve", lambda e: e.reciprocal(out=rz2[:], in_=rz2[:]), reads=[rz2b], writes=[rz2b])
                    kb.op("dve", lambda e: e.scalar_tensor_tensor(out=onb[:], in0=on0[:], scalar=subg[:, 0:1], in1=rz2[:],
                                                                  op0=ALU.mult, op1=ALU.mult),
                          reads=[on0b, subgb, rz2b], writes=[onbb])
                self.pending.append(subln)
                self.wo_accum(onb, onbb, 128, Wo[:, h % 2, :], Wob[h % 2], qc)
        self.flush_pending()
        kb.phase_end()

    def attn_B(self, i):
        kb = self.kb
        j = i // 3
        KV0, KR0 = 1536, 3584
        kb.phase_begin()
        self.alloc_norm_tmp()
        hT = kb.sb("b_hT", [128, 2, 8, 512], BF16)
        hTb = [kb.buf("b_hT0"), kb.buf("b_hT1")]
        Wa = kb.sb("b_Wa", [128, 8, 672], BF16)
        Wqb = kb.sb("b_Wqb", [128, 3, 1536], BF16)
        Wkvb = kb.sb("b_Wkvb", [128, 2, 2048], BF16)
        Wab, Wqbb, Wkvbb = kb.buf("Wa"), kb.buf("Wqb"), kb.buf("Wkvb")
        lT = kb.sb("b_lT", [128, 5, 512], BF16)
        lTb = [kb.buf("lT%d" % k) for k in range(4)]
        af2 = kb.sb("b_af", [128, 2, 672], F32)
        af2b = [kb.buf("af0"), kb.buf("af1")]
        latn2 = kb.sb("b_latn", [128, 2, 640], BF16)
        latn2b = [kb.buf("latn0"), kb.buf("latn1")]
        st2 = kb.sb("b_st", [128, 2, 4], F32)
        st2b = [kb.buf("st0"), kb.buf("st1")]
        bg = kb.sb("b_bg", [128, 5], F32)
        bgb = kb.buf("bg")
        ob = kb.sb("b_ob", [128, 3, 512], BF16)
        obb = [kb.buf("bob%d" % k) for k in range(3)]
        okr = kb.sb("b_okr", [128, 2, 32], BF16)
        okrb = [kb.buf("okr0"), kb.buf("okr1")]
        self.rope_alloc()
        rope_b2 = kb.sb("rope_b", [128, 2, 4, 128], F32)
        rope_b2b = [kb.buf("rope_b0"), kb.buf("rope_b1")]
        kb.dma("sp", bg[:], self.b_gain, self.ld_sem[9], writes=[bgb])
        kb.dma("pool", Wa[:], self.b_w_a[j].rearrange("(fc p) n -> p fc n", p=128), self.wsl_sem[0], writes=[Wab])
        kb.dma("pool", Wqb[:], self.b_w_qb[j].rearrange("(kc p) n -> p kc n", p=128), self.wsl_sem[1], writes=[Wqbb])
        kb.dma("pool", Wkvb[:], self.b_w_kvb[j].rearrange("(kc p) n -> p kc n", p=128), self.wsl_sem[2], writes=[Wkvbb])
        kr_i = 0
        self.norm_transpose_chunk(0, i, hT[:, 0], hTb[0])
        tile_i = 0
        for c in range(S // 512):
            hcur = hT[:, c % 2]
            hcurb = hTb[c % 2]
            rope_b = rope_b2[:, c % 2]
            rope_bb = rope_b2b[c % 2]
            kb.dma("sp", rope_b.rearrange("p t k -> p (t k)"), self.c_rope_b[:, c * 512:(c + 1) * 512], self.rsem[c % 2],
                   writes=[rope_bb])
            for tl in range(4):
                tt = c * 4 + tl
                zz = tile_i % 2
                tile_i += 1
                af, afb = af2[:, zz], af2b[zz]
                latn, latnb = latn2[:, zz], latn2b[zz]
                st, stb = st2[:, zz], st2b[zz]
                p1 = self.pp_i % 4
                p2 = (self.pp_i + 1) % 4
                self.pp_i += 2
                for fc in range(8):
                    kb.op("pe", lambda e: e.matmul(self.pb[p1][:], hcur[:, fc, tl * 128:(tl + 1) * 128], Wa[:, fc, 0:512],
                                                   start=(fc == 0), stop=(fc == 7)),
                          reads=[hcurb, Wab], writes=[self.pbb[p1]])
                for fc in range(8):
                    kb.op("pe", lambda e: e.matmul(self.pb[p2][:, 0:160], hcur[:, fc, tl * 128:(tl + 1) * 128],
                                                   Wa[:, fc, 512:672], start=(fc == 0), stop=(fc == 7)),
                          reads=[hcurb, Wab], writes=[self.pbb[p2]])
                kb.op("act", lambda e: e.activation(out=af[:, 0:512], in_=self.pb[p1][:], func=AF.Copy),
                      reads=[self.pbb[p1]], writes=[afb])
                kb.op("act", lambda e: e.activation(out=af[:, 512:672], in_=self.pb[p2][:, 0:160], func=AF.Copy),
                      reads=[self.pbb[p2]], writes=[afb])
                kb.op("dve", lambda e: e.memset(st[:, 0:2], 0.0), writes=[stb])
                kb.op("act", lambda e: e.activation(out=self.junk[:, 0:384], in_=af[:, 0:384], func=AF.Square,
                                                    accum_out=st[:, 0:1]), reads=[afb], writes=[self.junkb, stb])
                kb.op("act", lambda e: e.activation(out=self.junk[:, 0:256], in_=af[:, 384:640], func=AF.Square,
                                                    accum_out=st[:, 1:2]), reads=[afb], writes=[self.junkb, stb])
                kb.op("act", lambda e: e.activation(out=st[:, 0:1], in_=st[:, 0:1], func=AF.Sqrt, scale=1.0 / 384,
                                                    bias=self.epsc[:, 0:1]), reads=[stb, self.epscb], writes=[stb])
                kb.op("act", lambda e: e.activation(out=st[:, 1:2], in_=st[:, 1:2], func=AF.Sqrt, scale=1.0 / 256,
                                                    bias=self.epsc[:, 0:1]), reads=[stb, self.epscb], writes=[stb])
                kb.op("dve", lambda e: e.reciprocal(out=st[:, 2:4], in_=st[:, 0:2]), reads=[stb], writes=[stb])
                kb.op("dve", lambda e: e.tensor_scalar(out=latn[:, 0:384], in0=af[:, 0:384], scalar1=st[:, 2:3],
                                                       scalar2=None, op0=ALU.mult), reads=[afb, stb], writes=[latnb])
                kb.op("dve", lambda e: e.tensor_scalar(out=latn[:, 384:640], in0=af[:, 384:640], scalar1=st[:, 3:4],
                                                       scalar2=None, op0=ALU.mult), reads=[afb, stb], writes=[latnb])
                ks = kr_i % 2
                kr_i += 1
                self.rope_post(af[:, 640:672], 32, 32, 0, 16, rope_b[:, tl, :], rope_bb, okr[:, ks, :], okrb[ks], afb)
                kb.dma("sp", self.scr[tt * 128:(tt + 1) * 128, KR0:KR0 + 32], okr[:, ks, :], self.kr_sem[ks],
                       reads=[okrb[ks]])
                jj = self.tr_i % 2
                self.tr_i += 1
                for kc in range(5):
                    kb.op("pe", lambda e: e.transpose(self.ptrs[jj][:, kc * 128:(kc + 1) * 128],
                                                      latn[:, kc * 128:(kc + 1) * 128], self.ident[:]),
                          reads=[latnb, self.identb], writes=[self.ptrb[jj]])
                for kc in range(5):
                    kb.op("dve", lambda e: e.tensor_scalar(out=lT[:, kc, tl * 128:(tl + 1) * 128],
                                                           in0=self.ptrs[jj][:, kc * 128:(kc + 1) * 128],
                                                           scalar1=bg[:, kc:kc + 1], scalar2=None, op0=ALU.mult),
                          reads=[self.ptrb[jj], bgb], writes=[lTb[tl]])
                for g4 in range(4):
                    p = self.pp_i % 4
                    self.pp_i += 1
                    for kc in range(3):
                        kb.op("pe", lambda e: e.matmul(self.pb[p][:, 0:384], lT[:, kc, tl * 128:(tl + 1) * 128],
                                                       Wqb[:, kc, g4 * 384:(g4 + 1) * 384], start=(kc == 0), stop=(kc == 2)),
                              reads=[lTb[tl], Wqbb], writes=[self.pbb[p]])
                    os_ = self.ob_i % 3
                    self.ob_i += 1
                    self.rope_post(self.pb[p][:, 0:384], 384, 96, 64, 16, rope_b[:, tl, :], rope_bb,
                                   ob[:, os_, 0:384], obb[os_], self.pbb[p])
                    kb.dma("sp", self.scr[tt * 128:(tt + 1) * 128, g4 * 384:(g4 + 1) * 384], ob[:, os_, 0:384],
                           self.osl_sem[os_], reads=[obb[os_]])
                for g4 in range(4):
                    p = self.pp_i % 4
                    self.pp_i += 1
                    for kc in range(2):
                        kb.op("pe", lambda e: e.matmul(self.pb[p][:], lT[:, 3 + kc, tl * 128:(tl + 1) * 128],
                                                       Wkvb[:, kc, g4 * 512:(g4 + 1) * 512], start=(kc == 0), stop=(kc == 1)),
                              reads=[lTb[tl], Wkvbb], writes=[self.pbb[p]])
                    os_ = self.ob_i % 3
                    self.ob_i += 1
                    kb.op("act", lambda e: e.activation(out=ob[:, os_, :], in_=self.pb[p][:], func=AF.Copy),
                          reads=[self.pbb[p]], writes=[obb[os_]])
                    kb.dma("sp", self.scr[tt * 128:(tt + 1) * 128, KV0 + g4 * 512:KV0 + (g4 + 1) * 512], ob[:, os_, :],
                           self.osl_sem[os_], reads=[obb[os_]])
                if tl == 0 and c + 1 < S // 512:
                    self.norm_transpose_chunk(c + 1, i, hT[:, (c + 1) % 2], hTb[(c + 1) % 2])
        kb.phase_end()
        kb.phase_begin()
        if self.do_ffn:
            self.ffn_prep(i)
        QT = kb.sb("b_QT", [128, S], BF16)
        KT = kb.sb("b_KT", [128, S], BF16)
        V = kb.sb("b_V", [128, NT, 128], BF16)
        stg = kb.sb("b_stg", [128, NT, 96], BF16)
        QTb = [kb.buf("QT%d" % k) for k in range(8)]
        KTb = [kb.buf("KT%d" % k) for k in range(8)]
        Vb = [kb.buf("V%d" % k) for k in range(4)]
        stgb = [[kb.buf("stg%d" % k) for k in range(4)], [kb.buf("stg2_%d" % k) for k in range(4)]]
        stgK = kb.sb("b_stgK", [128, NT, 96], BF16)
        stgKb = [[kb.buf("stgK%d" % k) for k in range(4)], [kb.buf("stgK2_%d" % k) for k in range(4)]]
        pt = kb.sb("b_pt", [128, 3, 1024], BF16)
        ptb = [kb.buf("pt%d" % k) for k in range(3)]
        Wo = kb.sb("b_Wo", [64, 2, D], BF16)
        Wob = [kb.buf("Wo0"), kb.buf("Wo1")]
        oz = kb.sb("b_oz", [128, 512], F32)
        onb = kb.sb("b_onb", [64, 512], BF16)
        ozb, onbb = kb.buf("oz"), kb.buf("onb")
        kb.op("dve", lambda e: e.memset(V[:, :, 64:128], 1.0), writes=Vb)
        wo_d = self.b_w_o[j]
        scale = 96.0 ** -0.5
        for h in range(16):
            self.load_T(QT, QTb, [(h * 96, 96)], stg, stgb, 0)
            self.load_T(KT, KTb, [(KV0 + h * 128, 64), (KR0, 32)], stgK, stgKb, 16)
            self.load_V(V, Vb, KV0 + h * 128 + 64, 64)
            kb.dma("pool", Wo[:, h % 2, :], wo_d[h * 64:(h + 1) * 64, :], self.wo_sem[h % 2], writes=[Wob[h % 2]])
            for qc in range(8):
                items = [(KT[0:96, kt * 128:(kt + 1) * 128], QT[0:96, qc * 512:(qc + 1) * 512],
                          [KTb[kt // 4], QTb[qc]], kt, None) for kt in range(NT)]
                self.dense_core(items, scale, 128, pt, ptb, lambda kt: (V[:, kt, :], Vb[kt // 8]), False)
                self.flush_pending()
                kb.op("act", lambda e: e.activation(out=oz[:], in_=self.pb[4][:], func=AF.Copy), reads=[self.pbb[4]], writes=[ozb])
                self.norm_shift(oz, ozb, onb[:], onbb, defer=True)
                self.wo_accum(onb, onbb, 64, Wo[:, h % 2, :], Wob[h % 2], qc)
        self.flush_pending()
        kb.phase_end()

    def attn_C(self, i):
        kb = self.kb
        j = i // 3
        self.proj_simple(i, self.c_w_qkv[j], 18, lambda cg: (cg % 6) < 4)
        kb.phase_begin()
        if self.do_ffn:
            self.ffn_prep(i)
        QT = kb.sb("c_QT", [128, S], BF16)
        KT = kb.sb("c_KT", [128, S], BF16)
        V = kb.sb("c_V", [128, NT, 128], BF16)
        stg = kb.sb("c_stg", [128, NT, 64], BF16)
        QTb = [kb.buf("QT%d" % k) for k in range(8)]
        KTb = [kb.buf("KT%d" % k) for k in range(8)]
        Vb = [kb.buf("V%d" % k) for k in range(4)]
        stgb = [[kb.buf("stg%d" % k) for k in range(4)], [kb.buf("stg2_%d" % k) for k in range(4)]]
        stgK = kb.sb("c_stgK", [128, NT, 64], BF16)
        stgKb = [[kb.buf("stgK%d" % k) for k in range(4)], [kb.buf("stgK2_%d" % k) for k in range(4)]]
        pt = kb.sb("c_pt", [128, 3, 1024], BF16)
        ptb = [kb.buf("pt%d" % k) for k in range(3)]
        Wo = kb.sb("c_Wo", [64, 2, D], BF16)
        Wob = [kb.buf("Wo0"), kb.buf("Wo1")]
        onb_all = kb.sb("c_onball", [64, S], BF16)
        onball_b = [kb.buf("onball%d" % q) for q in range(8)]
        ctmp = kb.sb("c_tmp", [128, 512], F32)
        ctmpb = kb.buf("ctmp")
        acc = kb.sb("c_acc", [128, S], F32)
        accb = [kb.buf("acc%d" % q) for q in range(8)]
        kb.op("dve", lambda e: e.memset(V[:, :, 64:128], 1.0), writes=Vb)
        mk = [kb.sb("c_mk%d" % g, [128, C_MASKS[g][0].shape[1]], BF16) for g in range(3)]
        mkb = [kb.buf("mk%d" % g) for g in range(3)]
        for g in range(3):
            kb.dma("pool", mk[g][:], self.c_mask[g], self.mk_sem[g], writes=[mkb[g]])
        wo_d = self.c_w_o[j]
        moff = [0]
        for m in C_MASKS:
            moff.append(moff[-1] + len(m))
        for h in range(16):
            kb.dma("pool", Wo[:, h % 2, :], wo_d[h * 64:(h + 1) * 64, :], self.wo_sem[h % 2], writes=[Wob[h % 2]])
            for g in range(3):
                base = g * 3072 + h * 64
                svg = self.scr.rearrange("(h p ph) c -> p ph h c", h=2, p=128, ph=16) if g == 2 else None
                self.load_T(QT, QTb, [(base, 64)], stg, stgb, 0, sv=svg, pm=(g == 2))
                self.load_T(KT, KTb, [(base + 1024, 64)], stgK, stgKb, 16, sv=svg, pm=(g == 2))
                self.load_V(V, Vb, base + 2048, 64, sv=svg, pm=(g == 2))
                for qc in range(8):
                    items = []
                    for kt in (range(NT) if g < 2 else range(4 * qc, 4 * qc + 4)):
                        o = kt * 128 - qc * 512
                        mi = C_MIDX[g].get(o) if g < 2 else C_MIDX[2][kt - 4 * qc]
                        if mi is None:
                            continue
                        items.append((KT[0:64, kt * 128:(kt + 1) * 128], QT[0:64, qc * 512:(qc + 1) * 512],
                                      [KTb[kt // 4], QTb[qc]], kt, (mk[g][:, mi:mi + 512], mkb[g])))
                    self.dense_core(items, 0.125, 128, pt, ptb, lambda kt: (V[:, kt, :], Vb[kt // 8]), False)
                    sl = slice(qc * 512, (qc + 1) * 512)
                    if g == 0:
                        kb.op("act", lambda e: e.activation(out=acc[:, sl], in_=self.pb[4][:], func=AF.Copy),
                              reads=[self.pbb[4]], writes=[accb[qc]])
                    elif g == 2:
                        kb.op("act", lambda e: e.activation(out=ctmp[:], in_=self.pb[4][:], func=AF.Copy),
                              reads=[self.pbb[4]], writes=[ctmpb])
                        av = acc[:].rearrange("p (m ph) -> p m ph", ph=16)[:, :, 2 * qc:2 * qc + 2]
                        tv = ctmp[:].rearrange("p (ph m) -> p m ph", ph=2)
                        kb.op("dve", lambda e: e.tensor_tensor(out=av, in0=tv, in1=av, op=ALU.add),
                              reads=[ctmpb] + accb, writes=accb)
                    else:
                        kb.op("dve", lambda e: e.tensor_tensor(out=acc[:, sl], in0=self.pb[4][:], in1=acc[:, sl],
                                                               op=ALU.add), reads=[self.pbb[4], accb[qc]], writes=[accb[qc]])
            self.flush_pending()
            for qc in range(8):
                sl = slice(qc * 512, (qc + 1) * 512)
                self.norm_shift(acc[:, sl], accb[qc], onb_all[:, sl], onball_b[qc])
                self.wo_accum(onb_all[:, sl], onball_b[qc], 64, Wo[:, h % 2, :], Wob[h % 2], qc)
        self.flush_pending()
        kb.phase_end()

    def build(self, do_attn=True, do_ffn=True):
        self.do_ffn = do_ffn
        self.load_x()
        self.s_i = 0
        self.pt_i = 0
        self.o_i = 0
        self.pending = []
        if do_attn:
            self.attn_setup()
        for i in range(self.l0, self.l1):
            if do_attn:
                getattr(self, 'attn_' + 'ABC'[i % 3])(i)
            if do_ffn:
                self.ffn(i)
        self.store_out()
        return self.nc


def _gains(attn_norm, ffn_norm, final_norm):
    allg = np.concatenate([attn_norm, ffn_norm, final_norm.reshape(1, D)], axis=0)
    return np.ascontiguousarray(allg.reshape(9, 8, 128).transpose(2, 0, 1).reshape(128, 72)).astype(np.float32)


def _build_c_masks():
    groups = ((128, 1), (512, 4), (2048, 16))
    i = np.arange(128)[:, None]
    jq = np.arange(512)[None, :]
    masks, midx = [], []
    for (w, d) in groups:
        R = w // 2
        uniq, keys, idx = [], {}, {}
        for o in range(-2048, 2049, 128):
            delta = jq - i - o
            cond = (np.abs(delta) <= R) & (delta % d == 0)
            if not cond.any():
                continue
            m = np.where(cond, 0.0, -30000.0).astype(np.float32)
            kbytes = m.tobytes()
            if kbytes not in keys:
                keys[kbytes] = len(uniq)
                uniq.append(m)
            idx[o] = keys[kbytes]
        omin, omax = min(idx), max(idx)
        wdt = 512 + omax - omin
        cc = np.arange(wdt)[None, :]
        dl = cc - omax - i
        base = np.where((np.abs(dl) <= R) & (dl % d == 0), 0.0, -30000.0).astype(np.float32)
        for o in idx:
            assert np.array_equal(base[:, omax - o:omax - o + 512], uniq[idx[o]])
        masks.append([base])
        midx.append({o: omax - o for o in idx})
    jj = np.arange(512)[None, :]
    ii = np.arange(128)[:, None]
    uniq, idx = [], {}
    for r in range(4):
        cond = ((jj // 256) == (r // 2)) & (np.abs((jj % 256) - ((r % 2) * 128 + ii)) <= 64)
        uniq.append(np.where(cond, 0.0, -30000.0).astype(np.float32))
        idx[r] = r
    masks[2] = [np.concatenate(uniq, axis=1)]
    midx[2] = {r: 512 * r for r in range(4)}
    return masks, midx


C_MASKS, C_MIDX = _build_c_masks()


def _rope_tok(rot):
    pos = np.arange(S, dtype=np.float32)
    inv = (ROPE_THETA ** (-np.arange(0, rot, 2, dtype=np.float32) / rot)).astype(np.float32)
    ang = pos[:, None] * inv[None, :]
    rep = 64 // (rot // 2)
    t = np.concatenate([np.tile(np.cos(ang), (1, rep)), np.tile(np.sin(ang), (1, rep))], axis=1).astype(np.float32)
    return np.ascontiguousarray(t.reshape(NT, 128, 128).transpose(1, 0, 2).reshape(128, NT * 128))


def _consts():
    sel = np.zeros((128, 64), np.float32)
    sel[64 + np.arange(64), np.arange(64)] = 1.0
    return {"c_sel": sel, "c_ident": np.eye(128, dtype=np.float32), "c_rope_p": _rope_tok(16), "c_rope_b": _rope_tok(32)}


def _layout_inputs(inp):
    d = {"gains": _gains(inp["attn_norm"], inp["ffn_norm"], inp["final_norm"]),
         "final_norm": np.ascontiguousarray(inp["final_norm"].reshape(1, D)),
         "f_w_gu": inp["f_w_gu"], "f_w_out": inp["f_w_out"],
         "a_w_qkv": inp["a_w_qkv"], "a_w_o": inp["a_w_o"],
         "a_lam": np.ascontiguousarray(np.stack([inp["a_lambda_q1"], inp["a_lambda_k1"], inp["a_lambda_q2"],
                                                 inp["a_lambda_k2"]], axis=1)),
         "a_subln": np.ascontiguousarray(inp["a_subln"].reshape(2, 128, 1)),
         "b_w_a": inp["b_w_a"], "b_w_qb": inp["b_w_qb"], "b_w_kvb": inp["b_w_kvb"], "b_w_o": inp["b_w_o"],
         "b_gain": np.ascontiguousarray(np.concatenate([inp["b_q_norm"][0], inp["b_kv_norm"][0]]).reshape(5, 128).T),
         "c_w_qkv": inp["c_w_qkv"], "c_w_o": inp["c_w_o"],
         "c_mask0": C_MASKS[0][0], "c_mask1": C_MASKS[1][0], "c_mask2": C_MASKS[2][0]}
    d.update(_consts())
    return d


def kernel(**inputs):
    inp = {k: np.asarray(v, dtype=np.float32) for k, v in inputs.items()}
    n = inp["x"].shape[0]
    nc = Prog(0, DEPTH, True).build()
    shared = _layout_inputs(inp)
    in_maps = []
    for b in range(n):
        m = dict(shared)
        m["x"] = np.ascontiguousarray(inp["x"][b])
        in_maps.append(m)
    res = run_bass_kernel_spmd(nc, in_maps, core_ids=list(range(n)))
    return np.stack([np.asarray(r["y"], dtype=np.float32) for r in res.results], axis=0)
```
